# Optimizing a Trainium2 kernel written in Bass

```python
import math
import jax, jax.numpy as jnp
from jax import lax
import numpy as np

D_MODEL = 2048
BATCH = 4
SEQ = 4096
DEPTH = 1

MLA_HEADS = 8
MLA_Q_RANK = 512
MLA_KV_RANK = 512
MLA_NOPE_DIM = 128
MLA_ROPE_DIM = 64
MLA_V_DIM = 128
MLA_QK_DIM = MLA_NOPE_DIM + MLA_ROPE_DIM
MLA_V_COLS = MLA_HEADS * MLA_V_DIM
ROPE_THETA = 10000.0

DIFF_HEADS = 4
DIFF_QK_DIM = 128
DIFF_V_DIM = 2 * DIFF_QK_DIM
DIFF_QK_COLS = DIFF_HEADS * 2 * DIFF_QK_DIM
DIFF_V_COLS = DIFF_HEADS * DIFF_V_DIM

D_FF = 5632

IN_SPLIT_SIZES = (MLA_Q_RANK, MLA_KV_RANK, MLA_ROPE_DIM, DIFF_QK_COLS, DIFF_QK_COLS, DIFF_V_COLS, D_MODEL, D_MODEL)
IN_COLS = sum(IN_SPLIT_SIZES)

BLOCK_Q = 128
LN_EPS = 1e-5
RMS_EPS = 1e-6
ALPHA = (2 * DEPTH) ** 0.25
BETA = (8 * DEPTH) ** -0.25

kernel_name = 'hybrid_mla_diffattn_macaron_deepnorm'


def _layer_norm(x, g, b):
    xf = x.astype(jnp.float32)
    mu = jnp.mean(xf, axis=-1, keepdims=True)
    var = jnp.mean(jnp.square(xf - mu), axis=-1, keepdims=True)
    return ((xf - mu) * lax.rsqrt(var + LN_EPS) * g.astype(jnp.float32) + b.astype(jnp.float32)).astype(x.dtype)


def _rms_norm(x, g):
    xf = x.astype(jnp.float32)
    ms = jnp.mean(jnp.square(xf), axis=-1, keepdims=True)
    return (xf * lax.rsqrt(ms + RMS_EPS) * g.astype(jnp.float32)).astype(x.dtype)


def _swiglu(x, w_gate, w_up, w_down):
    return (jax.nn.silu(x @ w_gate) * (x @ w_up)) @ w_down


def _rope(x, cos, sin):
    half = x.shape[-1] // 2
    xf = x.astype(jnp.float32)
    x1, x2 = xf[..., :half], xf[..., half:]
    return jnp.concatenate([x1 * cos - x2 * sin, x1 * sin + x2 * cos], axis=-1).astype(x.dtype)


def _causal_mask(q0, q1):
    return jnp.arange(q0, q1)[:, None] >= jnp.arange(q1)[None, :]


def _split_columns(proj):
    outs, start = [], 0
    for size in IN_SPLIT_SIZES:
        outs.append(proj[..., start:start + size])
        start += size
    return outs


def _mla_attention(q, k, v):
    scale = q.shape[-1] ** -0.5
    outs = []
    for i in range(q.shape[1] // BLOCK_Q):
        q0, q1 = i * BLOCK_Q, (i + 1) * BLOCK_Q
        s = jnp.einsum('bqhd,bkhd->bhqk', q[:, q0:q1], k[:, :q1]).astype(jnp.float32) * scale
        s = jnp.where(_causal_mask(q0, q1), s, -jnp.inf)
        p = jax.nn.softmax(s, axis=-1).astype(v.dtype)
        outs.append(jnp.einsum('bhqk,bkhd->bqhd', p, v[:, :q1]))
    return jnp.concatenate(outs, axis=1)


def _diff_attention(q, k, v, positions, lam):
    scale = DIFF_QK_DIM ** -0.5
    slopes = 2.0 ** (-8.0 * jnp.arange(1, DIFF_HEADS + 1, dtype=jnp.float32) / DIFF_HEADS)
    pos = positions.astype(jnp.float32)
    outs = []
    for i in range(q.shape[1] // BLOCK_Q):
        q0, q1 = i * BLOCK_Q, (i + 1) * BLOCK_Q
        s = jnp.einsum('bqhmd,bkhmd->bhmqk', q[:, q0:q1], k[:, :q1]).astype(jnp.float32) * scale
        dist = jnp.abs(pos[:, q0:q1, None] - pos[:, None, :q1])
        s = s - slopes[None, :, None, None, None] * dist[:, None, None, :, :]
        s = jnp.where(_causal_mask(q0, q1), s, -jnp.inf)
        p = jax.nn.softmax(s, axis=-1)
        a = (p[:, :, 0] - lam * p[:, :, 1]).astype(v.dtype)
        outs.append(jnp.einsum('bhqk,bkhd->bqhd', a, v[:, :q1]))
    return jnp.concatenate(outs, axis=1)


def _mixer(h, positions, w_in, q_norm_g, w_uq, kv_norm_g, w_uk, w_uv,
           lq1, lk1, lq2, lk2, subln_g, w_br_mla, w_br_diff, w_out, lambda_init):
    b, s, _ = h.shape
    c_q, c_kv, k_r, dq, dk, dv, gate_mla, gate_diff = _split_columns(h @ w_in)

    half = MLA_ROPE_DIM // 2
    inv_freq = ROPE_THETA ** (-jnp.arange(half, dtype=jnp.float32) / half)
    ang = positions.astype(jnp.float32)[:, :, None, None] * inv_freq
    cos, sin = jnp.cos(ang), jnp.sin(ang)
    q = (_rms_norm(c_q, q_norm_g) @ w_uq).reshape(b, s, MLA_HEADS, MLA_QK_DIM)
    q = jnp.concatenate([q[..., :MLA_NOPE_DIM], _rope(q[..., MLA_NOPE_DIM:], cos, sin)], axis=-1)
    c_kv = _rms_norm(c_kv, kv_norm_g)
    k_nope = (c_kv @ w_uk).reshape(b, s, MLA_HEADS, MLA_NOPE_DIM)
    v_mla = (c_kv @ w_uv).reshape(b, s, MLA_HEADS, MLA_V_DIM)
    k_rope = _rope(k_r[:, :, None, :], cos, sin)
    k = jnp.concatenate([k_nope, jnp.broadcast_to(k_rope, (b, s, MLA_HEADS, MLA_ROPE_DIM))], axis=-1)
    o_mla = _mla_attention(q, k, v_mla).reshape(b, s, MLA_V_COLS)

    f32 = jnp.float32
    lam = (jnp.exp(jnp.sum(lq1.astype(f32) * lk1.astype(f32)))
           - jnp.exp(jnp.sum(lq2.astype(f32) * lk2.astype(f32))) + lambda_init)
    o_diff = _diff_attention(dq.reshape(b, s, DIFF_HEADS, 2, DIFF_QK_DIM),
                             dk.reshape(b, s, DIFF_HEADS, 2, DIFF_QK_DIM),
                             dv.reshape(b, s, DIFF_HEADS, DIFF_V_DIM), positions, lam)
    o_diff = (_rms_norm(o_diff, subln_g) * (1.0 - lambda_init)).reshape(b, s, DIFF_V_COLS)

    y = jax.nn.sigmoid(gate_mla) * (o_mla @ w_br_mla) + jax.nn.sigmoid(gate_diff) * (o_diff @ w_br_diff)
    return y @ w_out


def setup_inputs(seed: int = 0) -> dict:
    key = jax.random.key(seed)
    ks = iter(jax.random.split(key, 40))
    f32 = jnp.float32

    def w(shape, fan_in, scale=1.0):
        return jax.random.normal(next(ks), shape, f32) * (scale * fan_in ** -0.5)

    def gain(n):
        return 1.0 + 0.02 * jax.random.normal(next(ks), (DEPTH, n), f32)

    def bias(n):
        return 0.02 * jax.random.normal(next(ks), (DEPTH, n), f32)

    x = jax.random.normal(next(ks), (BATCH, SEQ, D_MODEL), f32)
    positions = jnp.broadcast_to(jnp.arange(SEQ, dtype=jnp.int32), (BATCH, SEQ))
    col_scale = jnp.concatenate([
        jnp.ones((IN_COLS - DIFF_V_COLS - 2 * D_MODEL,), f32),
        jnp.full((DIFF_V_COLS,), BETA, f32),
        jnp.ones((2 * D_MODEL,), f32)])
    return {
        'x': x,
        'positions': positions,
        'ln1_g': gain(D_MODEL),
        'ln1_b': bias(D_MODEL),
        'ffn1_w_gate': w((DEPTH, D_MODEL, D_FF), D_MODEL, BETA),
        'ffn1_w_up': w((DEPTH, D_MODEL, D_FF), D_MODEL, BETA),
        'ffn1_w_down': w((DEPTH, D_FF, D_MODEL), D_FF, BETA),
        'w_in': w((DEPTH, D_MODEL, IN_COLS), D_MODEL) * col_scale,
        'mla_q_norm_g': gain(MLA_Q_RANK),
        'mla_w_uq': w((DEPTH, MLA_Q_RANK, MLA_HEADS * MLA_QK_DIM), MLA_Q_RANK),
        'mla_kv_norm_g': gain(MLA_KV_RANK),
        'mla_w_uk': w((DEPTH, MLA_KV_RANK, MLA_HEADS * MLA_NOPE_DIM), MLA_KV_RANK),
        'mla_w_uv': w((DEPTH, MLA_KV_RANK, MLA_V_COLS), MLA_KV_RANK, BETA),
        'diff_lambda_q1': 0.1 * jax.random.normal(next(ks), (DEPTH, DIFF_QK_DIM), f32),
        'diff_lambda_k1': 0.1 * jax.random.normal(next(ks), (DEPTH, DIFF_QK_DIM), f32),
        'diff_lambda_q2': 0.1 * jax.random.normal(next(ks), (DEPTH, DIFF_QK_DIM), f32),
        'diff_lambda_k2': 0.1 * jax.random.normal(next(ks), (DEPTH, DIFF_QK_DIM), f32),
        'diff_subln_g': gain(DIFF_V_DIM),
        'w_branch_mla': w((DEPTH, MLA_V_COLS, D_MODEL), MLA_V_COLS, BETA),
        'w_branch_diff': w((DEPTH, DIFF_V_COLS, D_MODEL), DIFF_V_COLS, BETA),
        'w_out': w((DEPTH, D_MODEL, D_MODEL), D_MODEL, BETA),
        'ln2_g': gain(D_MODEL),
        'ln2_b': bias(D_MODEL),
        'ffn2_w_gate': w((DEPTH, D_MODEL, D_FF), D_MODEL, BETA),
        'ffn2_w_up': w((DEPTH, D_MODEL, D_FF), D_MODEL, BETA),
        'ffn2_w_down': w((DEPTH, D_FF, D_MODEL), D_FF, BETA),
        'ln3_g': gain(D_MODEL),
        'ln3_b': bias(D_MODEL),
    }


def reference(x, positions, ln1_g, ln1_b, ffn1_w_gate, ffn1_w_up, ffn1_w_down, w_in,
              mla_q_norm_g, mla_w_uq, mla_kv_norm_g, mla_w_uk, mla_w_uv,
              diff_lambda_q1, diff_lambda_k1, diff_lambda_q2, diff_lambda_k2, diff_subln_g,
              w_branch_mla, w_branch_diff, w_out, ln2_g, ln2_b,
              ffn2_w_gate, ffn2_w_up, ffn2_w_down, ln3_g, ln3_b):
    h = x
    for l in range(DEPTH):
        lambda_init = 0.8 - 0.6 * math.exp(-0.3 * l)
        h = _layer_norm(ALPHA * h + 0.5 * _swiglu(h, ffn1_w_gate[l], ffn1_w_up[l], ffn1_w_down[l]),
                        ln1_g[l], ln1_b[l])
        mix = _mixer(h, positions, w_in[l], mla_q_norm_g[l], mla_w_uq[l], mla_kv_norm_g[l],
                     mla_w_uk[l], mla_w_uv[l], diff_lambda_q1[l], diff_lambda_k1[l],
                     diff_lambda_q2[l], diff_lambda_k2[l], diff_subln_g[l],
                     w_branch_mla[l], w_branch_diff[l], w_out[l], lambda_init)
        h = _layer_norm(ALPHA * h + mix, ln2_g[l], ln2_b[l])
        h = _layer_norm(ALPHA * h + 0.5 * _swiglu(h, ffn2_w_gate[l], ffn2_w_up[l], ffn2_w_down[l]),
                        ln3_g[l], ln3_b[l])
    return h
```

```python
import math
import os
from contextlib import ExitStack

import numpy as np
import concourse.bass as bass
import concourse.mybir as mybir
from concourse.bass_utils import run_bass_kernel_spmd

F32 = mybir.dt.float32
BF16 = mybir.dt.bfloat16
I32 = mybir.dt.int32
ALU = mybir.AluOpType
AF = mybir.ActivationFunctionType

D = 2048
FF = 5632
SEQ = 4096
NTOK = 2048
TT = 512
NT = 4
KC = 16
FCH = 44
ALPHA = 2.0 ** 0.25
LN_EPS = 1e-5
RMS_EPS = 1e-6
LAMBDA_INIT = 0.8 - 0.6 * math.exp(-0.3 * 0)
SC_MLA = 192.0 ** -0.5
SC_DIFF = 128.0 ** -0.5
SLOPES = [2.0 ** (-8.0 * (h + 1) / 4) for h in range(4)]
TILES = ([0, 3, 4, 7], [1, 2, 5, 6])
SLOT_KEYS = [([], [0, 4]), ([0, 4], [5, 1]), ([0, 4, 5, 1], [2, 6]), ([0, 4, 5, 1, 2, 6], [7, 3])]
NSLOT_W = 4
EPOCH = 12000

C_CQ, C_CKV, C_KR, C_DQ, C_DK, C_DV, C_GM, C_GD = 0, 512, 1024, 1088, 2112, 3136, 4160, 6208

V_LN1G, V_LN1B, V_LN2G, V_LN2B, V_LN3G, V_LN3B = 0, 16, 32, 48, 64, 80
V_QG, V_KVG, V_SUBG, V_LAM, V_INVF, V_SGN = 96, 100, 104, 106, 110, 111
NV = 112


class Buf:
    __slots__ = ("name", "w", "r", "al", "excl")

    def __init__(self, name, excl=False):
        self.name = name
        self.excl = excl
        self.w = None
        self.r = []
        self.al = []


class Op:
    __slots__ = ("eng", "fn", "deps", "dma", "key", "sig", "cnt", "ep", "idx", "inc")


class Kern:
    ENGS = ("pe", "act", "dve", "pool", "sp")

    def __init__(self, nc, es):
        self.nc = nc
        self.es = es
        self.ops = []
        self.bufs = []
        self.engcnt = {e: 0 for e in self.ENGS}
        self.engep = {e: 0 for e in self.ENGS}
        self.engsem = {e: [] for e in self.ENGS}
        self.keysem = {}
        self.keycnt = {}
        self.lastkey = {}
        self.nsem = 0

    def buf(self, name, excl=False):
        b = Buf(name, excl)
        self.bufs.append(b)
        return b

    def sem(self, name):
        self.nsem += 1
        return self.es.enter_context(self.nc.semaphore(name))

    def op(self, eng, fn, reads=(), writes=(), dma=0, key=None, inc=16):
        o = Op()
        o.eng, o.fn, o.dma, o.key = eng, fn, dma, key
        o.inc = inc
        o.sig, o.cnt, o.ep = False, 0, 0
        deps = set()
        for b in reads:
            for bb in [b] + b.al:
                if bb.w is not None:
                    deps.add(bb.w)
                if bb.excl:
                    for r_ in bb.r:
                        if r_.eng != eng:
                            deps.add(r_)
        for b in writes:
            for bb in [b] + b.al:
                if bb.w is not None:
                    deps.add(bb.w)
                deps.update(bb.r)
        if dma:
            p = self.lastkey.get(key)
            if p is not None:
                deps.add(p)
            self.lastkey[key] = o
        for b in reads:
            b.r.append(o)
        for b in writes:
            b.w = o
            b.r = []
        deps.discard(o)
        o.deps = deps
        o.idx = len(self.ops)
        self.ops.append(o)
        return o

    def end_phase(self):
        last = {}
        dmas = []
        for o in self.ops:
            if o.dma:
                dmas.append(o)
            elif o.fn is not None:
                last[o.eng] = o
        lastd = {}
        for o in dmas:
            lastd[o.key] = o
        alld = set(last.values()) | set(lastd.values())
        for e in self.ENGS:
            o = Op()
            o.eng, o.fn, o.dma, o.key = e, None, 0, None
            o.inc = 16
            o.sig, o.cnt, o.ep = False, 0, 0
            o.deps = set(alld)
            o.idx = len(self.ops)
            self.ops.append(o)
        self._replay()
        self.ops = []
        self.lastkey = {}
        for b in self.bufs:
            b.w = None
            b.r = []

    def _replay(self):
        nc = self.nc
        ops = self.ops
        needed = set()
        for o in ops:
            for d in o.deps:
                if d.dma:
                    continue
                if d.eng == "pe" and o.eng == "pe":
                    continue
                needed.add(d)
        for o in ops:
            if o.dma:
                if o.key not in self.keysem:
                    self.keysem[o.key] = self.sem("k%d" % len(self.keysem))
                    self.keycnt[o.key] = 0
                self.keycnt[o.key] += o.inc * o.dma
                o.cnt = self.keycnt[o.key]
            elif o in needed:
                e = o.eng
                if self.engcnt[e] >= EPOCH or not self.engsem[e]:
                    self.engsem[e].append(self.sem("e%s%d" % (e, len(self.engsem[e]))))
                    self.engep[e] = len(self.engsem[e]) - 1
                    self.engcnt[e] = 0
                self.engcnt[e] += 1
                o.sig, o.cnt, o.ep = True, self.engcnt[e], self.engep[e]
        byeng = {e: [o for o in ops if o.eng == e] for e in self.ENGS}

        def run(e, name):
            seen_e = {}
            seen_k = {}
            for o in byeng[name]:
                we = {}
                wk = {}
                for d in o.deps:
                    if d.dma:
                        if wk.get(d.key, 0) < d.cnt:
                            wk[d.key] = d.cnt
                    else:
                        if d.eng == "pe" and name == "pe":
                            continue
                        v = (d.ep, d.cnt)
                        if we.get(d.eng, (-1, 0)) < v:
                            we[d.eng] = v
                for de, v in we.items():
                    if seen_e.get(de, (-1, 0)) >= v:
                        continue
                    seen_e[de] = v
                    e.wait_ge(self.engsem[de][v[0]], v[1])
                for k, c in wk.items():
                    if seen_k.get(k, 0) >= c:
                        continue
                    seen_k[k] = c
                    e.wait_ge(self.keysem[k], c)
                if o.fn is None:
                    continue
                ins = o.fn(e)
                if o.dma:
                    if not isinstance(ins, (list, tuple)):
                        ins = [ins]
                    assert len(ins) == o.dma
                    for i_ in ins:
                        i_.then_inc(self.keysem[o.key], o.inc)
                elif o.sig:
                    ins.then_inc(self.engsem[name][o.ep], 1)

        with nc.Block() as block:
            @block.tensor
            def _(e):
                run(e, "pe")

            @block.scalar
            def _(e):
                run(e, "act")

            @block.vector
            def _(e):
                run(e, "dve")

            @block.gpsimd
            def _(e):
                run(e, "pool")

            @block.sync
            def _(e):
                run(e, "sp")


class WStream:
    def __init__(self, K, slots, sbufs):
        self.K = K
        self.slots = slots
        self.sbufs = sbufs
        self.plan = []
        self.issued = 0
        self.taken = 0
        self.freed = 0

    def add(self, tag, srcs, shape):
        self.plan.append((tag, srcs, shape))

    def _issue(self, i):
        tag, srcs, (nk, ncols) = self.plan[i]
        s = i % len(self.slots)
        view = self.slots[s][:, 0:nk * ncols].rearrange("p (a b) -> p a b", a=nk)
        pairs = [(dst(view), src) for dst, src in srcs]

        def fn(e, pairs=pairs):
            return [e.dma_start(out=o_, in_=i_) for o_, i_ in pairs]

        self.K.op("pool", fn, writes=[self.sbufs[s]], dma=len(pairs), key=("wr", s))

    def _pump(self):
        while self.issued < len(self.plan) and self.issued - len(self.slots) < self.freed:
            self._issue(self.issued)
            self.issued += 1

    def next(self, tag):
        i = self.taken
        assert self.plan[i][0] == tag, (self.plan[i][0], tag)
        self._pump()
        assert self.issued > i, "weight ring deadlock: block %d (%s) not loadable" % (i, tag)
        self.taken += 1
        _, _, (nk, ncols) = self.plan[i]
        s = i % len(self.slots)
        view = self.slots[s][:, 0:nk * ncols].rearrange("p (a b) -> p a b", a=nk)
        return view, self.sbufs[s]

    def release(self, n=1):
        self.freed += n
        assert self.freed <= self.taken
        self._pump()


def _full(v):
    return v


def build_program(debug=False, stop_after=99, cut=99, ntile=NT, cut2=99):
    nc = bass.Bass("TRN2", target_bir_lowering=False)
    es = ExitStack()
    K = Kern(nc, es)

    def din(name, shape, dt=F32):
        return nc.dram_tensor(name, list(shape), dt, kind="ExternalInput").ap()

    x_d = din("x", [NTOK, D])
    qposb_d = din("qposb", [128, NTOK], I32)
    kposc_d = din("kposc", [128, 32], I32)
    qidxb_d = din("qidxb", [128, NTOK])
    kidxc_d = din("kidxc", [128, 32])
    vecs_d = din("vecs", [128, NV])
    ident_d = din("ident", [128, 128])
    w1g_d = din("ffn1_w_gate", [D, FF])
    w1u_d = din("ffn1_w_up", [D, FF])
    w1d_d = din("ffn1_w_down", [FF, D])
    win_d = din("w_in", [D, 8256])
    wuq_d = din("mla_w_uq", [512, 1536])
    wuk_d = din("mla_w_uk", [512, 1024])
    wuv_d = din("mla_w_uv", [512, 1024])
    wbm_d = din("w_branch_mla", [1024, D])
    wbd_d = din("w_branch_diff", [1024, D])
    wo_d = din("w_out", [D, D])
    w2g_d = din("ffn2_w_gate", [D, FF])
    w2u_d = din("ffn2_w_up", [D, FF])
    w2d_d = din("ffn2_w_down", [FF, D])
    out_d = nc.dram_tensor("out", [NTOK, D], F32, kind="ExternalOutput").ap()

    dbg = {}

    def dscr(name, shape, dt):
        if debug:
            t = nc.dram_tensor(name, list(shape), dt, kind="ExternalOutput")
            dbg[name] = t
            return t
        return nc.dram_tensor(name, list(shape), dt)

    H1A = dscr("H1A", [128, KC * NTOK], F32).ap().rearrange("p (c t) -> p c t", c=KC)
    H1B = dscr("H1B", [128, KC * NTOK], BF16).ap().rearrange("p (c t) -> p c t", c=KC)
    QN = dscr("QN", [128, 8 * NTOK], BF16).ap().rearrange("p (c t) -> p c t", c=8)
    QR = dscr("QR", [64, 8 * NTOK], BF16).ap().rearrange("p (c t) -> p c t", c=8)
    DQ = dscr("DQ", [128, 8 * NTOK], BF16).ap().rearrange("p (c t) -> p c t", c=8)
    OM = dscr("OM", [128, 8 * NTOK], BF16).ap().rearrange("p (c t) -> p c t", c=8)
    OD = dscr("OD", [128, 8 * NTOK], BF16).ap().rearrange("p (c t) -> p c t", c=8)
    KNX_t = [nc.dram_tensor("KNX%d" % i, [128, 4 * NTOK], BF16) for i in range(2)]
    DKX_t = [nc.dram_tensor("DKX%d" % i, [128, 4 * NTOK], BF16) for i in range(2)]
    VMX_t = [nc.dram_tensor("VMX%d" % i, [NTOK, 512], BF16) for i in range(2)]
    DVX_t = [nc.dram_tensor("DVX%d" % i, [NTOK, 512], BF16) for i in range(2)]
    KRX_t = nc.dram_tensor("KRX", [64, NTOK], BF16)
    KNG_t = [nc.dram_tensor("KNG%d" % i, [256, 4 * NTOK], BF16) for i in range(2)]
    DKG_t = [nc.dram_tensor("DKG%d" % i, [256, 4 * NTOK], BF16) for i in range(2)]
    VMG_t = [nc.dram_tensor("VMG%d" % i, [2 * NTOK, 512], BF16) for i in range(2)]
    DVG_t = [nc.dram_tensor("DVG%d" % i, [2 * NTOK, 512], BF16) for i in range(2)]
    KRG_t = nc.dram_tensor("KRG", [128, NTOK], BF16)
    KNX = [t_.ap().rearrange("p (c t) -> p c t", c=4) for t_ in KNX_t]
    DKX = [t_.ap().rearrange("p (c t) -> p c t", c=4) for t_ in DKX_t]
    VMX = [t_.ap() for t_ in VMX_t]
    DVX = [t_.ap() for t_ in DVX_t]
    KRX = KRX_t.ap()

    _nm = [0]

    def sb(name, shape, dt, stack=None):
        _nm[0] += 1
        return (stack or es).enter_context(nc.sbuf_tensor("s%d_%s" % (_nm[0], name), list(shape), dt))

    vecs = sb("vecs", [128, NV], F32)
    vecs2 = sb("vecs2", [128, 80], F32)
    ident = sb("ident", [128, 128], F32)
    ones_bf = sb("ones_bf", [128, 128], BF16)
    ones_f = sb("ones_f", [128, 128], F32)
    ident_bf = sb("ident_bf", [128, 128], BF16)
    lam = sb("lam", [128, 4], F32)
    b_const = K.buf("const")
    psum = [es.enter_context(nc.psum_tensor("ps%d" % i, [128, 512], F32)) for i in range(8)]
    psb = [K.buf("ps%d" % i, excl=True) for i in range(8)]

    def vcol(c, n=1, rows=128):
        return vecs[0:rows, c:c + n]

    K.op("sp", lambda e: [e.dma_start(out=vecs[:, :], in_=vecs_d[:, :]),
                          e.dma_start(out=ident[:, :], in_=ident_d[:, :])],
         writes=[b_const], dma=2, key="const")
    K.op("pool", lambda e: e.memset(ones_bf[:, :], 1.0), writes=[b_const])
    K.op("pool", lambda e: e.memset(ones_f[:, :], 1.0), writes=[b_const])
    K.op("dve", lambda e: e.tensor_copy(out=ident_bf[:, :], in_=ident[:, :]), reads=[b_const], writes=[b_const])
    K.op("dve", lambda e: e.tensor_scalar(out=vecs2[:, 0:64], in0=vecs[:, 0:64], scalar1=float(ALPHA),
                                          scalar2=None, op0=ALU.mult), reads=[b_const], writes=[b_const])
    K.op("dve", lambda e: e.tensor_tensor(out=vecs2[:, 64:65], in0=vcol(V_LAM), in1=vcol(V_LAM + 1), op=ALU.mult),
         reads=[b_const], writes=[b_const])
    K.op("dve", lambda e: e.tensor_tensor(out=vecs2[:, 65:66], in0=vcol(V_LAM + 2), in1=vcol(V_LAM + 3), op=ALU.mult),
         reads=[b_const], writes=[b_const])
    K.op("pe", lambda e: e.matmul(psum[0][:, 0:2], lhsT=ones_f[:, :], rhs=vecs2[:, 64:66], start=True, stop=True),
         reads=[b_const], writes=[psb[0]])
    K.op("act", lambda e: e.activation(out=vecs2[:, 66:68], in_=psum[0][:, 0:2], func=AF.Exp),
         reads=[psb[0]], writes=[b_const])
    K.op("dve", lambda e: e.tensor_tensor(out=lam[:, 0:1], in0=vecs2[:, 66:67], in1=vecs2[:, 67:68], op=ALU.subtract),
         reads=[b_const], writes=[b_const])
    K.op("dve", lambda e: e.tensor_scalar(out=lam[:, 0:1], in0=lam[:, 0:1], scalar1=float(LAMBDA_INIT), scalar2=None,
                                          op0=ALU.add), reads=[b_const], writes=[b_const])
    K.op("dve", lambda e: e.tensor_scalar(out=lam[:, 1:2], in0=lam[:, 0:1], scalar1=-1.0, scalar2=None,
                                          op0=ALU.mult), reads=[b_const], writes=[b_const])
    K.op("dve", lambda e: e.tensor_scalar(out=vecs2[:, 68:70], in0=vecs[:, V_SUBG:V_SUBG + 2],
                                          scalar1=float(1.0 - LAMBDA_INIT), scalar2=None, op0=ALU.mult),
         reads=[b_const], writes=[b_const])
    K.end_phase()
    if stop_after <= 0:
        es.close()
        return nc, dbg

    def wv(w, k0, k1, c0, c1):
        return w.rearrange("(kc p) n -> p kc n", p=128)[:, k0:k1, c0:c1]

    def ffn_plan(W, wg, wu, wd):
        for blk in range(11):
            W.add("g", [(_full, wv(wg, 0, 16, blk * 512, (blk + 1) * 512))], (16, 512))
            W.add("u", [(_full, wv(wu, 0, 16, blk * 512, (blk + 1) * 512))], (16, 512))
        for cb in range(8):
            W.add("d0", [(_full, wv(wd, 0, 32, cb * 256, (cb + 1) * 256))], (32, 256))
            W.add("d1", [(_full, wv(wd, 32, 44, cb * 256, (cb + 1) * 256))], (12, 256))

    class TileBufs:
        pass

    def alloc_tile_bufs(st):
        B = TileBufs()
        B.xb = sb("xb", [128, KC, TT], BF16, st)
        B.xa = sb("xa", [128, KC, TT], F32, st)
        B.act = sb("act", [128, FCH, TT], BF16, st)
        B.xbb = [K.buf("xb%d" % i) for i in range(KC)]
        B.xab = [K.buf("xa%d" % i) for i in range(KC)]
        B.actb = [K.buf("act%d" % i) for i in range(FCH)]
        B.tmpf = [sb("tmpf%d" % i, [128, TT], F32, st) for i in range(4)]
        B.tmpfb = [K.buf("tmpf%d" % i) for i in range(4)]
        B.tmpb = [sb("tmpb%d" % i, [128, TT], BF16, st) for i in range(4)]
        B.tmpbb = [K.buf("tmpb%d" % i) for i in range(4)]
        B.st = [sb("lnst%d" % i, [128, TT], F32, st) for i in range(3)]
        B.stb = [K.buf("lnst%d" % i) for i in range(3)]
        B.ctr = {"f": 0, "b": 0, "ps": 0}
        return B

    def ffn(B, W, mid_hook):
        for blk in range(11):
            wgv, wgb = W.next("g")
            wuv_, wub = W.next("u")
            for j in range(4):
                fc = blk * 4 + j
                pi = (fc % 4) * 2
                G, U = psum[pi], psum[pi + 1]
                for kc in range(KC):
                    K.op("pe", lambda e, G=G, kc=kc, j=j, wgv=wgv: e.matmul(
                        G[:, :], lhsT=wgv[:, kc, j * 128:(j + 1) * 128], rhs=B.xb[:, kc, :],
                        start=(kc == 0), stop=(kc == KC - 1)),
                        reads=[wgb, B.xbb[kc]], writes=[psb[pi]])
                for kc in range(KC):
                    K.op("pe", lambda e, U=U, kc=kc, j=j, wuv_=wuv_: e.matmul(
                        U[:, :], lhsT=wuv_[:, kc, j * 128:(j + 1) * 128], rhs=B.xb[:, kc, :],
                        start=(kc == 0), stop=(kc == KC - 1)),
                        reads=[wub, B.xbb[kc]], writes=[psb[pi + 1]])
                ti = B.ctr["f"] % 4
                B.ctr["f"] += 1
                tf, tfb = B.tmpf[ti], B.tmpfb[ti]
                K.op("act", lambda e, G=G, tf=tf: e.activation(out=tf[:, :], in_=G[:, :], func=AF.Silu),
                     reads=[psb[pi]], writes=[tfb])
                K.op("dve", lambda e, U=U, tf=tf, fc=fc: e.tensor_tensor(
                    out=B.act[:, fc, :], in0=tf[:, :], in1=U[:, :], op=ALU.mult),
                    reads=[tfb, psb[pi + 1]], writes=[B.actb[fc]])
            W.release(2)
        if mid_hook is not None:
            mid_hook()
        pend = []

        def stats(dc, first, last):
            tb1 = B.ctr["b"] % 4
            tb2 = (B.ctr["b"] + 1) % 4
            B.ctr["b"] += 2
            K.op("act", lambda e: e.activation(out=B.tmpb[tb1][:, :], in_=B.xa[:, dc, :], func=AF.Copy),
                 reads=[B.xab[dc]], writes=[B.tmpbb[tb1]])
            K.op("act", lambda e: e.activation(out=B.tmpb[tb2][:, :], in_=B.xa[:, dc, :], func=AF.Square),
                 reads=[B.xab[dc]], writes=[B.tmpbb[tb2]])
            pend.append((tb1, tb2, first, last))

        def flush_stats():
            while pend:
                tb1, tb2, first, last = pend.pop(0)
                K.op("pe", lambda e, tb1=tb1, first=first, last=last: e.matmul(
                    psum[4][:, :], lhsT=ones_bf[:, :], rhs=B.tmpb[tb1][:, :], start=first, stop=last),
                    reads=[B.tmpbb[tb1], b_const], writes=[psb[4]])
                K.op("pe", lambda e, tb2=tb2, first=first, last=last: e.matmul(
                    psum[5][:, :], lhsT=ones_bf[:, :], rhs=B.tmpb[tb2][:, :], start=first, stop=last),
                    reads=[B.tmpbb[tb2], b_const], writes=[psb[5]])

        for cb in range(8):
            w0, w0b = W.next("d0")
            w1, w1b = W.next("d1")
            base = (cb % 2) * 2
            for j in range(2):
                acc = psum[base + j]
                for kc in range(32):
                    K.op("pe", lambda e, acc=acc, kc=kc, j=j, w0=w0: e.matmul(
                        acc[:, :], lhsT=w0[:, kc, j * 128:(j + 1) * 128], rhs=B.act[:, kc, :],
                        start=(kc == 0), stop=False),
                        reads=[w0b, B.actb[kc]], writes=[psb[base + j]])
            for j in range(2):
                acc = psum[base + j]
                for kc in range(12):
                    K.op("pe", lambda e, acc=acc, kc=kc, j=j, w1=w1: e.matmul(
                        acc[:, :], lhsT=w1[:, kc, j * 128:(j + 1) * 128], rhs=B.act[:, 32 + kc, :],
                        start=False, stop=(kc == 11)),
                        reads=[w1b, B.actb[32 + kc]], writes=[psb[base + j]])
            W.release(2)
            flush_stats()
            for j in range(2):
                dc = cb * 2 + j
                acc = psum[base + j]
                K.op("dve", lambda e, acc=acc, dc=dc: e.scalar_tensor_tensor(
                    out=B.xa[:, dc, :], in0=acc[:, :], scalar=0.5, in1=B.xa[:, dc, :],
                    op0=ALU.mult, op1=ALU.add),
                    reads=[psb[base + j], B.xab[dc]], writes=[B.xab[dc]])
                stats(dc, dc == 0, dc == KC - 1)
        flush_stats()

    def ln_tail(B, gcol, bcol, ga, ba, want_b, want_a, final_plain=False):
        mu, var, rstd = B.st
        mub, varb, rstdb = B.stb
        K.op("dve", lambda e: e.tensor_scalar(out=mu[:, :], in0=psum[4][:, :], scalar1=1.0 / D, scalar2=None,
                                              op0=ALU.mult), reads=[psb[4]], writes=[mub])
        K.op("dve", lambda e: e.tensor_tensor(out=var[:, :], in0=mu[:, :], in1=mu[:, :], op=ALU.mult),
             reads=[mub], writes=[varb])
        K.op("dve", lambda e: e.scalar_tensor_tensor(out=var[:, :], in0=psum[5][:, :], scalar=1.0 / D, in1=var[:, :],
                                                     op0=ALU.mult, op1=ALU.subtract),
             reads=[psb[5], varb], writes=[varb])
        K.op("dve", lambda e: e.tensor_scalar(out=rstd[:, :], in0=var[:, :], scalar1=float(LN_EPS), scalar2=None,
                                              op0=ALU.add), reads=[varb], writes=[rstdb])
        K.op("act", lambda e: e.activation(out=rstd[:, :], in_=rstd[:, :], func=AF.Sqrt), reads=[rstdb], writes=[rstdb])
        K.op("dve", lambda e: e.reciprocal(out=rstd[:, :], in_=rstd[:, :]), reads=[rstdb], writes=[rstdb])
        for dc in range(KC):
            ti = B.ctr["f"] % 4
            B.ctr["f"] += 1
            tf, tfb = B.tmpf[ti], B.tmpfb[ti]
            K.op("dve", lambda e, tf=tf, dc=dc: e.tensor_tensor(out=tf[:, :], in0=B.xa[:, dc, :], in1=mu[:, :],
                                                                 op=ALU.subtract),
                 reads=[B.xab[dc], mub], writes=[tfb])
            K.op("dve", lambda e, tf=tf: e.tensor_tensor(out=tf[:, :], in0=tf[:, :], in1=rstd[:, :], op=ALU.mult),
                 reads=[tfb, rstdb], writes=[tfb])
            if want_b:
                K.op("act", lambda e, tf=tf, dc=dc: e.activation(
                    out=B.xb[:, dc, :], in_=tf[:, :], func=AF.Identity,
                    bias=vecs[:, bcol + dc:bcol + dc + 1], scale=vecs[:, gcol + dc:gcol + dc + 1]),
                    reads=[tfb, b_const], writes=[B.xbb[dc]])
            if want_a:
                src = vecs if final_plain else vecs2
                K.op("act", lambda e, tf=tf, dc=dc, src=src: e.activation(
                    out=B.xa[:, dc, :], in_=tf[:, :], func=AF.Identity,
                    bias=src[:, ba + dc:ba + dc + 1], scale=src[:, ga + dc:ga + dc + 1]),
                    reads=[tfb, b_const], writes=[B.xab[dc]])

    X_KN, X_VM, X_DK, X_DV, X_KR, X_H1B = [K.buf("x%d" % i) for i in range(6)]
    groups = [[0, 1], [2, 3], [4, 5], [6, 7]]
    pairs = [(KRX_t, KRG_t, X_KR)]
    for i_ in range(2):
        pairs += [(KNX_t[i_], KNG_t[i_], X_KN), (VMX_t[i_], VMG_t[i_], X_VM)]
    for i_ in range(2):
        pairs += [(DKX_t[i_], DKG_t[i_], X_DK), (DVX_t[i_], DVG_t[i_], X_DV)]
    gbufs = [K.buf("gath%d" % i) for i in range(len(pairs))]

    def issue_collectives():
        for i, (a, b, xb_) in enumerate(pairs):
            if cut2 == -1:
                break
            K.op("pool", lambda e, a=a, b=b: e.collective_compute(
                "AllGather", ALU.bypass, replica_groups=groups, ins=[a.ap().opt()], outs=[b.ap().opt()]),
                reads=[xb_], writes=[gbufs[i]], dma=1, key=("cc", i), inc=1)

    with ExitStack() as st:
        B = alloc_tile_bufs(st)
        wslots = [sb("wr%d" % i, [128, 8192], BF16, st) for i in range(NSLOT_W)]
        wsb = [K.buf("wr%d" % i) for i in range(NSLOT_W)]
        xin = [sb("xin%d" % i, [128, D // 2], F32, st) for i in range(2)]
        xinb = [K.buf("xin%d" % i) for i in range(2)]
        qpi = sb("qpi", [128, TT], I32, st)
        qpf = sb("qpf", [128, TT], F32, st)
        cosT = sb("cosT", [64, TT], F32, st)
        sinT = sb("sinT", [64, TT], F32, st)
        rang = sb("rang", [64, TT], F32, st)
        rtf = sb("rtf", [64, TT], F32, st)
        rti = sb("rti", [64, TT], I32, st)
        b_rope = K.buf("rope")
        cqn = sb("cqn", [128, 4, TT], BF16, st)
        cqnb = K.buf("cqn")
        stg = [B.act[:, 8 * i:8 * i + 8, :] for i in range(4)]
        stgb = [K.buf("stg%d" % i) for i in range(4)]
        for i in range(4):
            stgb[i].al = B.actb[8 * i:8 * i + 8]
        stgs = B.act[0:64, 32:34, :]
        stgsb = [K.buf("stgs0"), K.buf("stgs1")]
        stgsb[0].al = [B.actb[32]]
        stgsb[1].al = [B.actb[33]]
        for sb_ in stgb + stgsb:
            for ab_ in sb_.al:
                ab_.al.append(sb_)
        sctr = {"s": 0, "ss": 0, "ps": 0}

        W = WStream(K, wslots, wsb)
        for t in range(ntile):
            ffn_plan(W, w1g_d, w1u_d, w1d_d)
            W.add("ckv", [(_full, wv(win_d, 0, 16, C_CKV, C_CKV + 512))], (16, 512))
            for hb in range(2):
                W.add("dk", [(_full, wv(win_d, 0, 16, C_DK + hb * 512, C_DK + (hb + 1) * 512))], (16, 512))
            W.add("uk", [(_full, wv(wuk_d, 0, 4, 0, 1024))], (4, 1024))
            W.add("uv", [(_full, wv(wuv_d, 0, 4, 0, 1024))], (4, 1024))
            W.add("kr", [(lambda v: v[:, :, 0:64], wv(win_d, 0, 16, C_KR, C_KR + 64)),
                         (lambda v: v[:, :, 64:96], wv(win_d, 0, 16, C_KR + 32, C_KR + 64)),
                         (lambda v: v[:, :, 96:128], wv(win_d, 0, 16, C_KR, C_KR + 32))], (16, 128))
            for hb in range(2):
                W.add("dv", [(_full, wv(win_d, 0, 16, C_DV + hb * 512, C_DV + (hb + 1) * 512))], (16, 512))
            W.add("cq", [(_full, wv(win_d, 0, 16, C_CQ, C_CQ + 512))], (16, 512))
            for hb in range(2):
                W.add("dq", [(_full, wv(win_d, 0, 16, C_DQ + hb * 512, C_DQ + (hb + 1) * 512))], (16, 512))
            W.add("uq", [(_full, wv(wuq_d, 0, 4, 0, 1536))], (4, 1536))
            wuq_h = wuq_d.rearrange("(kc p) (h c) -> p kc h c", p=128, h=8)
            uqp_srcs = []
            for kc_ in range(4):
                uqp_srcs.append((lambda v, kc_=kc_: v[:, kc_, :].rearrange("p (h c) -> p h c", h=8)[:, :, 0:32],
                                 wuq_h[:, kc_, :, 160:192]))
                uqp_srcs.append((lambda v, kc_=kc_: v[:, kc_, :].rearrange("p (h c) -> p h c", h=8)[:, :, 32:64],
                                 wuq_h[:, kc_, :, 128:160]))
            W.add("uqp", uqp_srcs, (4, 512))

        def nps():
            i = sctr["ps"] % 8
            sctr["ps"] += 1
            return i

        def nstg():
            i = sctr["s"] % 4
            sctr["s"] += 1
            return i

        def evac_bf(eng, dst, src, reads, writes):
            if eng == "act":
                K.op("act", lambda e: e.activation(out=dst, in_=src, func=AF.Copy), reads=reads, writes=writes)
            else:
                K.op("dve", lambda e: e.tensor_copy(out=dst, in_=src), reads=reads, writes=writes)

        xpref = set()

        def tile1(t, part):
            tok0 = t * TT
            A = part == "A"
            if not A:
                K.op("sp", lambda e: e.dma_start(out=B.xb[:, :, :], in_=H1B[:, :, tok0:tok0 + TT]), reads=[X_H1B], writes=B.xbb,
                     dma=1, key="l_h1b")
            for s in range(4 if A else 0):
                for hf in range(2):
                    xi, xib = xin[hf], xinb[hf]
                    if (t, s, hf) not in xpref:
                        K.op("sp", lambda e, xi=xi, s=s, hf=hf: e.dma_start(
                            out=xi[:, :], in_=x_d[tok0 + s * 128:tok0 + (s + 1) * 128, hf * 1024:(hf + 1) * 1024]),
                            writes=[xib], dma=1, key=("xin", hf))
                    for q2 in range(2):
                        q4 = hf * 2 + q2
                        pi = nps()
                        for j in range(4):
                            K.op("pe", lambda e, pi=pi, j=j, q2=q2, xi=xi: e.transpose(
                                out=psum[pi][:, j * 128:(j + 1) * 128], in_=xi[:, (q2 * 4 + j) * 128:(q2 * 4 + j + 1) * 128],
                                identity=ident[:, :]), reads=[xib, b_const], writes=[psb[pi]])
                        pv = psum[pi][:, :].rearrange("p (a b) -> p a b", a=4)
                        K.op("act", lambda e, pv=pv, q4=q4, s=s: e.activation(
                            out=B.xa[:, q4 * 4:q4 * 4 + 4, s * 128:(s + 1) * 128], in_=pv, func=AF.Copy,
                            scale=float(ALPHA)), reads=[psb[pi]], writes=B.xab[q4 * 4:q4 * 4 + 4])
                        K.op("dve", lambda e, pv=pv, q4=q4, s=s: e.tensor_copy(
                            out=B.xb[:, q4 * 4:q4 * 4 + 4, s * 128:(s + 1) * 128], in_=pv),
                            reads=[psb[pi]], writes=B.xbb[q4 * 4:q4 * 4 + 4])
            K.op("sp", lambda e: e.dma_start(out=qpi[:, :], in_=qposb_d[:, tok0:tok0 + TT]),
                 writes=[b_rope], dma=1, key="qpi")
            K.op("dve", lambda e: e.tensor_copy(out=qpf[:, :], in_=qpi[:, :]), reads=[b_rope], writes=[b_rope])
            K.op("dve", lambda e: e.tensor_scalar(out=rang[:, :], in0=qpf[0:64, :], scalar1=vcol(V_INVF, 1, 64),
                                                  scalar2=None, op0=ALU.mult), reads=[b_rope, b_const], writes=[b_rope])
            for dstT, off in ((sinT, 0.0), (cosT, 0.5 * math.pi)):
                K.op("dve", lambda e, off=off: e.tensor_scalar(out=rtf[:, :], in0=rang[:, :], scalar1=float(off),
                                                               scalar2=float(1.0 / (2 * math.pi)), op0=ALU.add, op1=ALU.mult),
                     reads=[b_rope], writes=[b_rope])
                K.op("dve", lambda e: e.tensor_copy(out=rti[:, :], in_=rtf[:, :]), reads=[b_rope], writes=[b_rope])
                K.op("dve", lambda e: e.tensor_copy(out=rtf[:, :], in_=rti[:, :]), reads=[b_rope], writes=[b_rope])
                K.op("dve", lambda e: e.scalar_tensor_tensor(out=rtf[:, :], in0=rtf[:, :], scalar=float(-2 * math.pi),
                                                             in1=rang[:, :], op0=ALU.mult, op1=ALU.add),
                     reads=[b_rope], writes=[b_rope])
                K.op("dve", lambda e, off=off: e.tensor_scalar(out=rtf[:, :], in0=rtf[:, :], scalar1=float(off),
                                                               scalar2=float(-math.pi), op0=ALU.add, op1=ALU.max),
                     reads=[b_rope], writes=[b_rope])
                K.op("dve", lambda e: e.tensor_scalar(out=rtf[:, :], in0=rtf[:, :], scalar1=float(math.pi), scalar2=None,
                                                      op0=ALU.min), reads=[b_rope], writes=[b_rope])
                K.op("act", lambda e, dstT=dstT: e.activation(out=dstT[:, :], in_=rtf[:, :], func=AF.Sin),
                     reads=[b_rope], writes=[b_rope])
            K.op("dve", lambda e: e.tensor_scalar(out=sinT[:, :], in0=sinT[:, :], scalar1=vcol(V_SGN, 1, 64),
                                                  scalar2=None, op0=ALU.mult), reads=[b_rope, b_const], writes=[b_rope])
            if A:
                ffn(B, W, None)
            if debug and t == 0 and A:
                DZ = dscr("DBG_Z", [128, KC * TT], F32).ap().rearrange("p (c t) -> p c t", c=KC)
                DA = dscr("DBG_ACT", [128, FCH * TT], BF16).ap().rearrange("p (c t) -> p c t", c=FCH)
                DX = dscr("DBG_XB", [128, KC * TT], BF16).ap().rearrange("p (c t) -> p c t", c=KC)
                K.op("sp", lambda e: e.dma_start(out=DZ, in_=B.xa[:, :, :]), reads=B.xab, dma=1, key="dbgz")
                K.op("sp", lambda e: e.dma_start(out=DA, in_=B.act[:, :, :]), reads=B.actb, dma=1, key="dbga")
                K.op("sp", lambda e: e.dma_start(out=DX, in_=B.xb[:, :, :]), reads=B.xbb, dma=1, key="dbgx")
            if A:
                ln_tail(B, V_LN1G, V_LN1B, V_LN1G, V_LN1B, True, True)
                K.op("sp", lambda e: e.dma_start(out=H1B[:, :, tok0:tok0 + TT], in_=B.xb[:, :, :]),
                     reads=B.xbb, writes=[X_H1B], dma=1, key="h1b")
                K.op("sp", lambda e: e.dma_start(out=H1A[:, :, tok0:tok0 + TT], in_=B.xa[:, :, :]),
                     reads=B.xab, dma=1, key="h1a")
                if t + 1 < ntile:
                    for hf in range(2):
                        xpref.add((t + 1, 0, hf))
                        K.op("sp", lambda e, hf=hf: e.dma_start(
                            out=xin[hf][:, :], in_=x_d[tok0 + TT:tok0 + TT + 128, hf * 1024:(hf + 1) * 1024]),
                            writes=[xinb[hf]], dma=1, key=("xin", hf))

            def latent(tag, gcol):
                wv_, wb_ = W.next(tag)
                pis = []
                for c in range(4):
                    pi = nps()
                    pis.append(pi)
                    for kc in range(KC):
                        K.op("pe", lambda e, pi=pi, kc=kc, c=c: e.matmul(
                            psum[pi][:, :], lhsT=wv_[:, kc, c * 128:(c + 1) * 128], rhs=B.xb[:, kc, :],
                            start=(kc == 0), stop=(kc == KC - 1)), reads=[wb_, B.xbb[kc]], writes=[psb[pi]])
                    K.op("act", lambda e, pi=pi, c=c: e.activation(out=B.xa[:, c, :], in_=psum[pi][:, :], func=AF.Copy),
                         reads=[psb[pi]], writes=[B.xab[c]])
                    tb = B.ctr["b"] % 4
                    B.ctr["b"] += 1
                    K.op("act", lambda e, pi=pi, tb=tb: e.activation(out=B.tmpb[tb][:, :], in_=psum[pi][:, :], func=AF.Square),
                         reads=[psb[pi]], writes=[B.tmpbb[tb]])
                    pis.append(tb)
                W.release(1)
                ps_ss = nps()
                for c in range(4):
                    tb = pis[2 * c + 1]
                    K.op("pe", lambda e, tb=tb, c=c: e.matmul(psum[ps_ss][:, :], lhsT=ones_bf[:, :], rhs=B.tmpb[tb][:, :],
                                                             start=(c == 0), stop=(c == 3)),
                         reads=[B.tmpbb[tb], b_const], writes=[psb[ps_ss]])
                rinv, rinvb = B.st[0], B.stb[0]
                K.op("dve", lambda e: e.tensor_scalar(out=rinv[:, :], in0=psum[ps_ss][:, :], scalar1=1.0 / 512,
                                                      scalar2=float(RMS_EPS), op0=ALU.mult, op1=ALU.add),
                     reads=[psb[ps_ss]], writes=[rinvb])
                K.op("act", lambda e: e.activation(out=rinv[:, :], in_=rinv[:, :], func=AF.Sqrt), reads=[rinvb], writes=[rinvb])
                K.op("dve", lambda e: e.reciprocal(out=rinv[:, :], in_=rinv[:, :]), reads=[rinvb], writes=[rinvb])
                for c in range(4):
                    K.op("dve", lambda e, c=c: e.scalar_tensor_tensor(
                        out=cqn[:, c, :], in0=B.xa[:, c, :], scalar=vecs[:, gcol + c:gcol + c + 1], in1=rinv[:, :],
                        op0=ALU.mult, op1=ALU.mult), reads=[B.xab[c], rinvb, b_const], writes=[cqnb])

            def rope_out(pa, pb, dst, dstb):
                t1, t1b = B.tmpf[B.ctr["f"] % 4], B.tmpfb[B.ctr["f"] % 4]
                t2, t2b = B.tmpf[(B.ctr["f"] + 1) % 4], B.tmpfb[(B.ctr["f"] + 1) % 4]
                B.ctr["f"] += 2
                K.op("dve", lambda e: e.tensor_tensor(out=t1[0:64, :], in0=psum[pa][0:64, :], in1=cosT[:, :], op=ALU.mult),
                     reads=[psb[pa], b_rope], writes=[t1b])
                K.op("dve", lambda e: e.tensor_tensor(out=t2[0:64, :], in0=psum[pb][0:64, :], in1=sinT[:, :], op=ALU.mult),
                     reads=[psb[pb], b_rope], writes=[t2b])
                K.op("dve", lambda e: e.tensor_tensor(out=dst, in0=t1[0:64, :], in1=t2[0:64, :], op=ALU.add),
                     reads=[t1b, t2b], writes=[dstb])

            def q_path():
              wq, wqb = W.next("uq")
              wqp, wqpb = W.next("uqp")
              sq = nstg()
              for h in range(8):
                  pi = nps()
                  for kc in range(4):
                      K.op("pe", lambda e, pi=pi, kc=kc, h=h: e.matmul(
                          psum[pi][:, :], lhsT=wq[:, kc, h * 192:h * 192 + 128], rhs=cqn[:, kc, :],
                          start=(kc == 0), stop=(kc == 3)), reads=[wqb, cqnb], writes=[psb[pi]])
                  evac_bf("act" if h % 2 else "dve", stg[sq][:, h, :], psum[pi][:, :], [psb[pi]], [stgb[sq]])
              K.op("sp", lambda e, sq=sq: e.dma_start(out=QN[:, :, tok0:tok0 + TT], in_=stg[sq]),
                   reads=[stgb[sq]], dma=1, key="qn")
              for h in range(8):
                  pa, pb = nps(), nps()
                  for kc in range(4):
                      K.op("pe", lambda e, pa=pa, kc=kc, h=h: e.matmul(
                          psum[pa][0:64, :], lhsT=wq[:, kc, h * 192 + 128:h * 192 + 192], rhs=cqn[:, kc, :],
                          start=(kc == 0), stop=(kc == 3)), reads=[wqb, cqnb], writes=[psb[pa]])
                  for kc in range(4):
                      K.op("pe", lambda e, pb=pb, kc=kc, h=h: e.matmul(
                          psum[pb][0:64, :], lhsT=wqp[:, kc, h * 64:h * 64 + 64], rhs=cqn[:, kc, :],
                          start=(kc == 0), stop=(kc == 3)), reads=[wqpb, cqnb], writes=[psb[pb]])
                  ss = sctr["ss"] % 2
                  sctr["ss"] += 1
                  rope_out(pa, pb, stgs[:, ss, :], stgsb[ss])
                  K.op("sp", lambda e, ss=ss, h=h: e.dma_start(out=QR[:, h, tok0:tok0 + TT], in_=stgs[:, ss, :]),
                       reads=[stgsb[ss]], dma=1, key=("qr", ss))
              W.release(2)

            def kv_path():
              wk, wkb = W.next("uk")
              sk = nstg()
              for h in range(8):
                  pi = nps()
                  for kc in range(4):
                      K.op("pe", lambda e, pi=pi, kc=kc, h=h: e.matmul(
                          psum[pi][:, :], lhsT=wk[:, kc, h * 128:(h + 1) * 128], rhs=cqn[:, kc, :],
                          start=(kc == 0), stop=(kc == 3)), reads=[wkb, cqnb], writes=[psb[pi]])
                  evac_bf("act" if h % 2 else "dve", stg[sk][:, h, :], psum[pi][:, :], [psb[pi]], [stgb[sk]])
              K.op("sp", lambda e, sk=sk: [e.dma_start(out=KNX[i_][:, :, tok0:tok0 + TT], in_=stg[sk][:, 4 * i_:4 * i_ + 4, :])
                                           for i_ in range(2)], reads=[stgb[sk]], writes=[X_KN], dma=2, key="knx")
              W.release(1)
              wvv, wvb = W.next("uv")
              sv = nstg()
              svv = stg[sv].rearrange("p (s a) t -> p s (a t)", s=4)
              for s in range(4):
                  for hb in range(2):
                      pi = nps()
                      for kc in range(4):
                          K.op("pe", lambda e, pi=pi, kc=kc, s=s, hb=hb: e.matmul(
                              psum[pi][:, :], lhsT=cqn[:, kc, s * 128:(s + 1) * 128], rhs=wvv[:, kc, hb * 512:(hb + 1) * 512],
                              start=(kc == 0), stop=(kc == 3)), reads=[wvb, cqnb], writes=[psb[pi]])
                      evac_bf("act" if hb else "dve", svv[:, s, hb * 512:(hb + 1) * 512], psum[pi][:, :],
                              [psb[pi]], [stgb[sv]])
              K.op("sp", lambda e, svv=svv: [e.dma_start(
                  out=VMX[i_][tok0:tok0 + TT, :].rearrange("(s p) n -> p s n", p=128), in_=svv[:, :, i_ * 512:(i_ + 1) * 512])
                  for i_ in range(2)], reads=[stgb[sv]], writes=[X_VM], dma=2, key="vmx")
              W.release(1)
              wkr, wkrb = W.next("kr")
              pa, pb = nps(), nps()
              for kc in range(KC):
                  K.op("pe", lambda e, pa=pa, kc=kc: e.matmul(psum[pa][0:64, :], lhsT=wkr[:, kc, 0:64], rhs=B.xb[:, kc, :],
                                                              start=(kc == 0), stop=(kc == KC - 1)),
                       reads=[wkrb, B.xbb[kc]], writes=[psb[pa]])
              for kc in range(KC):
                  K.op("pe", lambda e, pb=pb, kc=kc: e.matmul(psum[pb][0:64, :], lhsT=wkr[:, kc, 64:128], rhs=B.xb[:, kc, :],
                                                              start=(kc == 0), stop=(kc == KC - 1)),
                       reads=[wkrb, B.xbb[kc]], writes=[psb[pb]])
              W.release(1)
              ss = sctr["ss"] % 2
              sctr["ss"] += 1
              rope_out(pa, pb, stgs[:, ss, :], stgsb[ss])
              K.op("sp", lambda e, ss=ss: e.dma_start(out=KRX[:, tok0:tok0 + TT], in_=stgs[:, ss, :]),
                   reads=[stgsb[ss]], writes=[X_KR], dma=1, key=("qr", ss))
            def dqk(nm, dst, key):
                sd = nstg()
                for hb in range(2):
                    wd_, wdb_ = W.next(nm)
                    for c in range(4):
                        pi = nps()
                        for kc in range(KC):
                            K.op("pe", lambda e, pi=pi, kc=kc, c=c, wd_=wd_: e.matmul(
                                psum[pi][:, :], lhsT=wd_[:, kc, c * 128:(c + 1) * 128], rhs=B.xb[:, kc, :],
                                start=(kc == 0), stop=(kc == KC - 1)), reads=[wdb_, B.xbb[kc]], writes=[psb[pi]])
                        evac_bf("act" if c % 2 else "dve", stg[sd][:, hb * 4 + c, :], psum[pi][:, :], [psb[pi]], [stgb[sd]])
                    W.release(1)
                if dst is not None:
                    K.op("sp", lambda e, sd=sd, dst=dst: e.dma_start(out=dst[:, :, tok0:tok0 + TT], in_=stg[sd]),
                         reads=[stgb[sd]], dma=1, key=key)
                else:
                    K.op("sp", lambda e, sd=sd: [e.dma_start(out=DKX[i_][:, :, tok0:tok0 + TT], in_=stg[sd][:, 4 * i_:4 * i_ + 4, :])
                                                 for i_ in range(2)], reads=[stgb[sd]], writes=[X_DK], dma=2, key=key)
            latent("ckv", V_KVG)
            dqk("dk", None, "dkx")
            kv_path()
            sv = nstg()
            svv = stg[sv].rearrange("p (s a) t -> p s (a t)", s=4)
            for hb in range(2 if A else 0):
                wd_, wdb_ = W.next("dv")
                for s in range(4):
                    pi = nps()
                    for kc in range(KC):
                        K.op("pe", lambda e, pi=pi, kc=kc, s=s, wd_=wd_: e.matmul(
                            psum[pi][:, :], lhsT=B.xb[:, kc, s * 128:(s + 1) * 128], rhs=wd_[:, kc, :],
                            start=(kc == 0), stop=(kc == KC - 1)), reads=[wdb_, B.xbb[kc]], writes=[psb[pi]])
                    evac_bf("act" if s % 2 else "dve", svv[:, s, hb * 512:(hb + 1) * 512], psum[pi][:, :],
                            [psb[pi]], [stgb[sv]])
                W.release(1)
            if A:
                K.op("sp", lambda e, svv=svv: [e.dma_start(
                    out=DVX[i_][tok0:tok0 + TT, :].rearrange("(s p) n -> p s n", p=128), in_=svv[:, :, i_ * 512:(i_ + 1) * 512])
                    for i_ in range(2)], reads=[stgb[sv]], writes=[X_DV], dma=2, key="dvx")
            latent("cq", V_QG)
            dqk("dq", DQ, "dqx")
            q_path()

        for t in range(ntile):
            tile1(t, "A")
        K.end_phase()
    if stop_after <= 1:
        es.close()
        return nc, dbg

    G_KR = gbufs[0]
    G_KN = [gbufs[1], gbufs[3]]
    G_VM = [gbufs[2], gbufs[4]]
    G_DK = [gbufs[5], gbufs[7]]
    G_DV = [gbufs[6], gbufs[8]]
    KNG = [t_.ap().rearrange("(r p) (c t) -> r p c t", r=2, c=4) for t_ in KNG_t]
    DKG = [t_.ap().rearrange("(r p) (c t) -> r p c t", r=2, c=4) for t_ in DKG_t]
    KRG = KRG_t.ap().rearrange("(r p) t -> r p t", r=2)
    VMG = [t_.ap().rearrange("(c p) n -> p c n", p=128) for t_ in VMG_t]
    DVG = [t_.ap().rearrange("(c p) n -> p c n", p=128) for t_ in DVG_t]

    issue_collectives()
    with ExitStack() as st:
        kb = [sb("kb%d" % i, [128, 2 * NTOK], BF16, st) for i in range(2)]
        kbb = [K.buf("kb%d" % i) for i in range(2)]
        vb = [sb("vb%d" % i, [128, 32, 256], BF16, st) for i in range(2)]
        vbb = [K.buf("vb%d" % i) for i in range(2)]
        qb = [sb("qb%d" % i, [128, NTOK], BF16, st) for i in range(2)]
        qbb = [K.buf("qb%d" % i) for i in range(2)]
        qrb = [sb("qrb%d" % i, [64, NTOK], BF16, st) for i in range(2)]
        qrbb = [K.buf("qrb%d" % i) for i in range(2)]
        krb = sb("krb", [64, 2 * NTOK], BF16, st)
        krbb = K.buf("krb")
        pen = sb("pen", [128, 32, TT], BF16, st)
        penb = K.buf("pen")
        qidx = sb("qidx", [128, NTOK], F32, st)
        kidx = sb("kidx", [128, 32], F32, st)
        qpi2 = sb("qpi2", [128, NTOK], I32, st)
        kpi2 = sb("kpi2", [128, 32], I32, st)
        qpos = sb("qpos", [128, NTOK], F32, st)
        kpos = sb("kpos", [128, 32], F32, st)
        b_idx = K.buf("idx")
        NDR = 6
        dring = [sb("dist%d" % i, [128, TT], F32, st) for i in range(NDR)]
        dringb = [K.buf("dist%d" % i) for i in range(NDR)]
        NSR = 4
        sring = [sb("sfix%d" % i, [128, TT], F32, st) for i in range(NSR)]
        sringb = [K.buf("sfix%d" % i) for i in range(NSR)]
        NPR = 4
        pring = [sb("pT%d" % i, [128, TT], BF16, st) for i in range(NPR)]
        pringb = [K.buf("pT%d" % i) for i in range(NPR)]
        fin = [sb("fin%d" % i, [128, TT], F32, st) for i in range(6)]
        finb = [K.buf("fin%d" % i) for i in range(6)]
        o1 = sb("o1", [128, 2, TT], F32, st)
        o1b = K.buf("o1")
        od = sb("od", [128, 2, TT], F32, st)
        odb = K.buf("od")
        ost = [sb("ost%d" % i, [128, 2, TT], BF16, st) for i in range(2)]
        ostb = [K.buf("ost%d" % i) for i in range(2)]
        ptmp = [sb("ptmp%d" % i, [128, TT], F32, st) for i in range(2)]
        ptmpb = [K.buf("ptmp%d" % i) for i in range(2)]
        ctr = {"d": 0, "s": 0, "p": 0, "o": 0, "S": 0, "acc": 0, "pt": 0}

        K.op("sp", lambda e: [e.dma_start(out=qidx[:, :], in_=qidxb_d[:, :]),
                              e.dma_start(out=kidx[:, :], in_=kidxc_d[:, :]),
                              e.dma_start(out=qpi2[:, :], in_=qposb_d[:, :]),
                              e.dma_start(out=kpi2[:, :], in_=kposc_d[:, :])],
             writes=[b_idx], dma=4, key="idx")
        K.op("dve", lambda e: e.tensor_copy(out=qpos[:, :], in_=qpi2[:, :]), reads=[b_idx], writes=[b_idx])
        K.op("dve", lambda e: e.tensor_copy(out=kpos[:, :], in_=kpi2[:, :]), reads=[b_idx], writes=[b_idx])
        K.op("dve", lambda e: e.tensor_scalar(out=kpos[:, :], in0=kpos[:, :], scalar1=-1.0, scalar2=None, op0=ALU.mult),
             reads=[b_idx], writes=[b_idx])
        pen_of = {}
        for j in range(4):
            for gi, g_ in enumerate(SLOT_KEYS[j][1]):
                for i in range(4):
                    c = g_ * 4 + i
                    pidx = j * 8 + gi * 4 + i
                    pen_of[(j, c)] = pidx
                    K.op("dve", lambda e, pidx=pidx, j=j, c=c: e.tensor_scalar(
                        out=pen[:, pidx, :], in0=qidx[:, j * TT:(j + 1) * TT], scalar1=kidx[:, c:c + 1],
                        scalar2=-30000.0, op0=ALU.is_lt, op1=ALU.mult), reads=[b_idx], writes=[penb])
        K.op("sp", lambda e: [e.dma_start(out=krb[:, r * NTOK:(r + 1) * NTOK], in_=KRG[r]) for r in range(2)],
             reads=[G_KR], writes=[krbb], dma=2, key="krb")

        def chunks_of(j):
            un, ma = SLOT_KEYS[j]
            return [(g_ * 4 + i, False) for g_ in un for i in range(4)] + \
                   [(g_ * 4 + i, True) for g_ in ma for i in range(4)]

        MLA_ACC = [(4, 5), (6, 7)]

        def mla_group(h, s, j, accO, accZ):
            tiles = chunks_of(j)
            n = len(tiles)
            sbank = {}
            qn_ = qb[s][:, j * TT:(j + 1) * TT]
            qr_ = qrb[s][:, j * TT:(j + 1) * TT]

            def emitS(i):
                c = tiles[i][0]
                bi = ctr["S"] % 4
                ctr["S"] += 1
                sbank[i] = bi
                K.op("pe", lambda e: e.matmul(psum[bi][:, :], lhsT=kb[s][:, c * 128:(c + 1) * 128], rhs=qn_, start=True, stop=False),
                     reads=[kbb[s], qbb[s]], writes=[psb[bi]])
                masked = tiles[i][1]
                K.op("pe", lambda e: e.matmul(psum[bi][:, :], lhsT=krb[:, c * 128:(c + 1) * 128], rhs=qr_, start=False,
                                              stop=(not masked)),
                     reads=[krbb, qrbb[s]], writes=[psb[bi]])
                if masked:
                    pidx = pen_of[(j, c)]
                    K.op("pe", lambda e: e.matmul(psum[bi][:, :], lhsT=ident_bf[:, :], rhs=pen[:, pidx, :], start=False, stop=True),
                         reads=[b_const, penb], writes=[psb[bi]])

            def emitP(i):
                c, masked = tiles[i]
                bi = sbank[i]
                pi = ctr["p"] % NPR
                ctr["p"] += 1
                src, srcb = psum[bi][:, :], [psb[bi]]
                K.op("act", lambda e: e.activation(out=pring[pi][:, :], in_=src, func=AF.Exp, scale=float(SC_MLA)),
                     reads=srcb, writes=[pringb[pi]])
                return pi

            def emitPV(i, pi):
                c = tiles[i][0]
                K.op("pe", lambda e: e.matmul(psum[accO][:, :], lhsT=vb[s][:, c, 0:128], rhs=pring[pi][:, :],
                                              start=(i == 0), stop=(i == n - 1)),
                     reads=[vbb[s], pringb[pi]], writes=[psb[accO]])
                K.op("pe", lambda e: e.matmul(psum[accZ][:, :], lhsT=ones_bf[:, :], rhs=pring[pi][:, :],
                                              start=(i == 0), stop=(i == n - 1)),
                     reads=[b_const, pringb[pi]], writes=[psb[accZ]])

            DEPTH = 3
            for i in range(min(DEPTH, n)):
                emitS(i)
            for i in range(n):
                pi = emitP(i)
                emitPV(i, pi)
                if i + DEPTH < n:
                    emitS(i + DEPTH)

        def load_mla_head(h):
            s = h % 2
            K.op("sp", lambda e: [e.dma_start(out=kb[s][:, r * NTOK:(r + 1) * NTOK], in_=KNG[h // 4][r][:, h % 4, :]) for r in range(2)],
                 reads=[G_KN[h // 4]], writes=[kbb[s]], dma=2, key=("kb", s))
            K.op("sp", lambda e: e.dma_start(out=vb[s][:, :, 0:128], in_=VMG[h // 4][:, :, (h % 4) * 128:(h % 4 + 1) * 128]),
                 reads=[G_VM[h // 4]], writes=[vbb[s]], dma=1, key=("vb", s))
            K.op("sp", lambda e: e.dma_start(out=qb[s][:, :], in_=QN[:, h, :]), writes=[qbb[s]], dma=1, key=("qb", s))
            K.op("sp", lambda e: e.dma_start(out=qrb[s][:, :], in_=QR[:, h, :]), writes=[qrbb[s]], dma=1, key=("qrb", s))

        if cut2 >= 2:
            load_mla_head(0)
        for h in range(8 if cut2 >= 3 else (1 if cut2 >= 2 else 0)):
            if h + 1 < 8:
                load_mla_head(h + 1)
            s = h % 2
            for j in range(4):
                accO, accZ = MLA_ACC[ctr["acc"] % 2]
                ctr["acc"] += 1
                mla_group(h, s, j, accO, accZ)
                fi = ctr["o"] % 6
                ctr["o"] += 1
                oi = ctr["o"] % 2
                K.op("dve", lambda e, fi=fi, accZ=accZ: e.reciprocal(out=fin[fi][:, :], in_=psum[accZ][:, :]),
                     reads=[psb[accZ]], writes=[finb[fi]])
                K.op("dve", lambda e, fi=fi, oi=oi, accO=accO: e.tensor_tensor(
                    out=ost[oi][:, 0, :], in0=psum[accO][:, :], in1=fin[fi][:, :], op=ALU.mult),
                    reads=[psb[accO], finb[fi]], writes=[ostb[oi]])
                K.op("sp", lambda e, oi=oi, h=h, j=j: e.dma_start(out=OM[:, h, j * TT:(j + 1) * TT], in_=ost[oi][:, 0, :]),
                     reads=[ostb[oi]], dma=1, key=("ost", oi))

        DACC = [(2, 3, 4), (5, 6, 7)]

        def load_diff_head(h):
            K.op("sp", lambda e: e.dma_start(out=vb[h % 2][:, :, :], in_=DVG[h // 2][:, :, (h % 2) * 256:(h % 2 + 1) * 256]),
                 reads=[G_DV[h // 2]], writes=[vbb[h % 2]], dma=1, key=("vb", h % 2))
            for m in range(2):
                hm = 2 * h + m
                K.op("sp", lambda e, m=m, hm=hm: [e.dma_start(out=kb[m][:, r * NTOK:(r + 1) * NTOK], in_=DKG[hm // 4][r][:, hm % 4, :])
                                                 for r in range(2)],
                     reads=[G_DK[hm // 4]], writes=[kbb[m]], dma=2, key=("kb", m))
                K.op("sp", lambda e, m=m, hm=hm: e.dma_start(out=qb[m][:, :], in_=DQ[:, hm, :]), writes=[qbb[m]], dma=1, key=("qb", m))

        def diff_group(h, j):
            tiles = chunks_of(j)
            n = len(tiles)
            cneg = -SLOPES[h] / SC_DIFF
            vs = h % 2
            dtile = {}

            def emitS(i):
                c = tiles[i][0]
                masked = tiles[i][1]
                for m in range(2):
                    K.op("pe", lambda e, m=m: e.matmul(psum[m][:, :], lhsT=kb[m][:, c * 128:(c + 1) * 128],
                                                       rhs=qb[m][:, j * TT:(j + 1) * TT], start=True, stop=(not masked)),
                         reads=[kbb[m], qbb[m]], writes=[psb[m]])
                    if masked:
                        pidx = pen_of[(j, c)]
                        K.op("pe", lambda e, m=m, pidx=pidx: e.matmul(psum[m][:, :], lhsT=ident_bf[:, :], rhs=pen[:, pidx, :],
                                                                     start=False, stop=True),
                             reads=[b_const, penb], writes=[psb[m]])

            def emitD(i):
                c = tiles[i][0]
                di = ctr["d"] % NDR
                ctr["d"] += 1
                dtile[i] = di
                K.op("act", lambda e: e.activation(out=dring[di][:, :], in_=qpos[:, j * TT:(j + 1) * TT], func=AF.Abs,
                                                   bias=kpos[:, c:c + 1], scale=1.0), reads=[b_idx], writes=[dringb[di]])

            def emitP(i):
                c, masked = tiles[i]
                di = dtile[i]
                pis = []
                for m in range(2):
                    si = ctr["s"] % NSR
                    ctr["s"] += 1
                    K.op("dve", lambda e, m=m, si=si: e.scalar_tensor_tensor(
                        out=sring[si][:, :], in0=dring[di][:, :], scalar=float(cneg), in1=psum[m][:, :],
                        op0=ALU.mult, op1=ALU.add), reads=[dringb[di], psb[m]], writes=[sringb[si]])
                    pi = ctr["p"] % NPR
                    ctr["p"] += 1
                    K.op("act", lambda e, si=si, pi=pi: e.activation(out=pring[pi][:, :], in_=sring[si][:, :], func=AF.Exp,
                                                                    scale=float(SC_DIFF)),
                         reads=[sringb[si]], writes=[pringb[pi]])
                    pis.append(pi)
                return pis

            def emitPV(i, pis):
                c = tiles[i][0]
                for m in range(2):
                    a0, a1, az = DACC[m]
                    for d_, bank in ((0, a0), (1, a1)):
                        K.op("pe", lambda e, m=m, d_=d_, bank=bank: e.matmul(
                            psum[bank][:, :], lhsT=vb[vs][:, c, d_ * 128:(d_ + 1) * 128], rhs=pring[pis[m]][:, :],
                            start=(i == 0), stop=(i == n - 1)), reads=[vbb[vs], pringb[pis[m]]], writes=[psb[bank]])
                    K.op("pe", lambda e, m=m, az=az: e.matmul(psum[az][:, :], lhsT=ones_bf[:, :], rhs=pring[pis[m]][:, :],
                                                             start=(i == 0), stop=(i == n - 1)),
                         reads=[b_const, pringb[pis[m]]], writes=[psb[az]])

            emitS(0)
            emitD(0)
            for i in range(n):
                if i + 1 < n:
                    emitD(i + 1)
                pis = emitP(i)
                if i == min(2, n - 1) and pending_fin:
                    pending_fin.pop(0)()
                if i + 1 < n:
                    emitS(i + 1)
                emitPV(i, pis)

            f1 = ctr["o"] % 6
            f2 = (ctr["o"] + 1) % 6
            f3 = (ctr["o"] + 2) % 6
            ctr["o"] += 3
            K.op("dve", lambda e: e.reciprocal(out=fin[f1][:, :], in_=psum[DACC[0][2]][:, :]),
                 reads=[psb[DACC[0][2]]], writes=[finb[f1]])
            K.op("dve", lambda e: e.reciprocal(out=fin[f2][:, :], in_=psum[DACC[1][2]][:, :]),
                 reads=[psb[DACC[1][2]]], writes=[finb[f2]])
            K.op("dve", lambda e: e.tensor_scalar(out=fin[f2][:, :], in0=fin[f2][:, :], scalar1=lam[:, 1:2], scalar2=None,
                                                  op0=ALU.mult), reads=[finb[f2], b_const], writes=[finb[f2]])
            for d_ in range(2):
                K.op("dve", lambda e, d_=d_: e.tensor_tensor(out=o1[:, d_, :], in0=psum[DACC[0][d_]][:, :], in1=fin[f1][:, :],
                                                            op=ALU.mult), reads=[psb[DACC[0][d_]], finb[f1]], writes=[o1b])
                K.op("dve", lambda e, d_=d_: e.tensor_tensor(out=od[:, d_, :], in0=psum[DACC[1][d_]][:, :], in1=fin[f2][:, :],
                                                            op=ALU.mult), reads=[psb[DACC[1][d_]], finb[f2]], writes=[odb])
            for d_ in range(2):
                K.op("dve", lambda e, d_=d_: e.tensor_tensor(out=od[:, d_, :], in0=od[:, d_, :], in1=o1[:, d_, :], op=ALU.add),
                     reads=[odb, o1b], writes=[odb])
            pending_fin.append(lambda: fin_part2(h, j, f3))

        def fin_part2(h, j, f3):
            tbs = []
            for d_ in range(2):
                pi = ctr["p"] % NPR
                ctr["p"] += 1
                tbs.append(pi)
                K.op("act", lambda e, pi=pi, d_=d_: e.activation(out=pring[pi][:, :], in_=od[:, d_, :], func=AF.Square),
                     reads=[odb], writes=[pringb[pi]])
            for d_ in range(2):
                K.op("pe", lambda e, d_=d_, pi=tbs[d_]: e.matmul(
                    psum[0][:, :], lhsT=ones_bf[:, :], rhs=pring[pi][:, :], start=(d_ == 0), stop=(d_ == 1)),
                    reads=[pringb[tbs[d_]], b_const], writes=[psb[0]])
            K.op("dve", lambda e: e.tensor_scalar(out=fin[f3][:, :], in0=psum[0][:, :], scalar1=1.0 / 256, scalar2=float(RMS_EPS),
                                                  op0=ALU.mult, op1=ALU.add), reads=[psb[0]], writes=[finb[f3]])
            K.op("act", lambda e: e.activation(out=fin[f3][:, :], in_=fin[f3][:, :], func=AF.Sqrt), reads=[finb[f3]], writes=[finb[f3]])
            K.op("dve", lambda e: e.reciprocal(out=fin[f3][:, :], in_=fin[f3][:, :]), reads=[finb[f3]], writes=[finb[f3]])
            oi = ctr["o"] % 2
            for d_ in range(2):
                K.op("dve", lambda e, d_=d_: e.scalar_tensor_tensor(
                    out=ost[oi][:, d_, :], in0=od[:, d_, :], scalar=vecs2[:, 68 + d_:69 + d_], in1=fin[f3][:, :],
                    op0=ALU.mult, op1=ALU.mult), reads=[odb, finb[f3], b_const], writes=[ostb[oi]])
            K.op("sp", lambda e: e.dma_start(out=OD[:, 2 * h:2 * h + 2, j * TT:(j + 1) * TT], in_=ost[oi][:, :, :]),
                 reads=[ostb[oi]], dma=1, key=("ost", oi))

        pending_fin = []
        for h in range(4 if cut2 >= 4 else 0):
            load_diff_head(h)
            for j in range(4):
                diff_group(h, j)
        while pending_fin:
            pending_fin.pop(0)()
        K.end_phase()

    if stop_after <= 2:
        es.close()
        return nc, dbg

    with ExitStack() as st:
        B = alloc_tile_bufs(st)
        wslots = [sb("wr%d" % i, [128, 8192], BF16, st) for i in range(NSLOT_W)]
        wsb = [K.buf("wr%d" % i) for i in range(NSLOT_W)]
        xout = [sb("xout%d" % i, [128, 512], F32, st) for i in range(2)]
        xoutb = [K.buf("xout%d" % i) for i in range(2)]
        om = sb("om", [128, 8, TT], BF16, st)
        omb = K.buf("om")
        odt = sb("odt", [128, 8, TT], BF16, st)
        odtb = K.buf("odt")
        ytmp = sb("ytmp", [128, 4, TT], F32, st)
        ytmpb = [K.buf("ytmp%d" % i) for i in range(4)]
        yT = B.act[:, 0:16, :]
        yb = B.actb[0:16]
        sctr = {"ps": 0, "xo": 0}

        W = WStream(K, wslots, wsb)
        for t in range(NT):
            for gb in range(4):
                W.add("gm", [(_full, wv(win_d, 0, 16, C_GM + gb * 512, C_GM + (gb + 1) * 512))], (16, 512))
                W.add("bm", [(_full, wv(wbm_d, 0, 8, gb * 512, (gb + 1) * 512))], (8, 512))
                W.add("gd", [(_full, wv(win_d, 0, 16, C_GD + gb * 512, C_GD + (gb + 1) * 512))], (16, 512))
                W.add("bd", [(_full, wv(wbd_d, 0, 8, gb * 512, (gb + 1) * 512))], (8, 512))
            for ob in range(4):
                W.add("wo", [(_full, wv(wo_d, 0, 16, ob * 512, (ob + 1) * 512))], (16, 512))
            ffn_plan(W, w2g_d, w2u_d, w2d_d)

        def nps():
            i = sctr["ps"] % 8
            sctr["ps"] += 1
            return i

        def load_inputs(t):
            t0 = t * TT
            K.op("sp", lambda e: e.dma_start(out=B.xb[:, :, :], in_=H1B[:, :, t0:t0 + TT]), writes=B.xbb, dma=1, key="l_h1b")
            K.op("sp", lambda e: e.dma_start(out=om[:, :, :], in_=OM[:, :, t0:t0 + TT]), writes=[omb], dma=1, key="l_om")
            K.op("sp", lambda e: e.dma_start(out=odt[:, :, :], in_=OD[:, :, t0:t0 + TT]), writes=[odtb], dma=1, key="l_od")

        def tile3(t):
            tok0 = t * TT
            if t == 0:
                load_inputs(0)
            K.op("sp", lambda e: e.dma_start(out=B.xa[:, :, :], in_=H1A[:, :, tok0:tok0 + TT]), writes=B.xab, dma=1, key="l_h1a")
            for gb in range(4):
                for half, (gt, bt, osrc, osrcb) in enumerate((("gm", "bm", om, omb), ("gd", "bd", odt, odtb))):
                    wg_, wgb_ = W.next(gt)
                    wb_, wbb_ = W.next(bt)
                    for c in range(4):
                        yc = gb * 4 + c
                        pg, pb_ = nps(), nps()
                        for kc in range(KC):
                            K.op("pe", lambda e, pg=pg, kc=kc, c=c, wg_=wg_: e.matmul(
                                psum[pg][:, :], lhsT=wg_[:, kc, c * 128:(c + 1) * 128], rhs=B.xb[:, kc, :],
                                start=(kc == 0), stop=(kc == KC - 1)), reads=[wgb_, B.xbb[kc]], writes=[psb[pg]])
                        for kc in range(8):
                            K.op("pe", lambda e, pb_=pb_, kc=kc, c=c, wb_=wb_, osrc=osrc: e.matmul(
                                psum[pb_][:, :], lhsT=wb_[:, kc, c * 128:(c + 1) * 128], rhs=osrc[:, kc, :],
                                start=(kc == 0), stop=(kc == 7)), reads=[wbb_, osrcb], writes=[psb[pb_]])
                        ti = B.ctr["f"] % 4
                        B.ctr["f"] += 1
                        tf, tfb = B.tmpf[ti], B.tmpfb[ti]
                        K.op("act", lambda e, pg=pg, tf=tf: e.activation(out=tf[:, :], in_=psum[pg][:, :], func=AF.Sigmoid),
                             reads=[psb[pg]], writes=[tfb])
                        if half == 0:
                            K.op("dve", lambda e, pb_=pb_, tf=tf, c=c: e.tensor_tensor(
                                out=ytmp[:, c, :], in0=tf[:, :], in1=psum[pb_][:, :], op=ALU.mult),
                                reads=[tfb, psb[pb_]], writes=[ytmpb[c]])
                        else:
                            K.op("dve", lambda e, pb_=pb_, tf=tf: e.tensor_tensor(
                                out=tf[:, :], in0=tf[:, :], in1=psum[pb_][:, :], op=ALU.mult),
                                reads=[tfb, psb[pb_]], writes=[tfb])
                            K.op("dve", lambda e, tf=tf, c=c, yc=yc: e.tensor_tensor(
                                out=yT[:, yc, :], in0=tf[:, :], in1=ytmp[:, c, :], op=ALU.add),
                                reads=[tfb, ytmpb[c]], writes=[yb[yc]])
                    W.release(2)
            pend = []
            for ob in range(4):
                wo_, wob_ = W.next("wo")
                for c in range(4):
                    dc = ob * 4 + c
                    pi = nps() % 4
                    for kc in range(KC):
                        K.op("pe", lambda e, pi=pi, kc=kc, c=c, wo_=wo_: e.matmul(
                            psum[pi][:, :], lhsT=wo_[:, kc, c * 128:(c + 1) * 128], rhs=yT[:, kc, :],
                            start=(kc == 0), stop=(kc == KC - 1)), reads=[wob_, yb[kc]], writes=[psb[pi]])
                    while pend:
                        tb1, tb2, first, last = pend.pop(0)
                        K.op("pe", lambda e, tb1=tb1, first=first, last=last: e.matmul(
                            psum[4][:, :], lhsT=ones_bf[:, :], rhs=B.tmpb[tb1][:, :], start=first, stop=last),
                            reads=[B.tmpbb[tb1], b_const], writes=[psb[4]])
                        K.op("pe", lambda e, tb2=tb2, first=first, last=last: e.matmul(
                            psum[5][:, :], lhsT=ones_bf[:, :], rhs=B.tmpb[tb2][:, :], start=first, stop=last),
                            reads=[B.tmpbb[tb2], b_const], writes=[psb[5]])
                    K.op("dve", lambda e, pi=pi, dc=dc: e.tensor_tensor(out=B.xa[:, dc, :], in0=psum[pi][:, :], in1=B.xa[:, dc, :],
                                                                        op=ALU.add),
                         reads=[psb[pi], B.xab[dc]], writes=[B.xab[dc]])
                    tb1 = B.ctr["b"] % 4
                    tb2 = (B.ctr["b"] + 1) % 4
                    B.ctr["b"] += 2
                    K.op("act", lambda e, tb1=tb1, dc=dc: e.activation(out=B.tmpb[tb1][:, :], in_=B.xa[:, dc, :], func=AF.Copy),
                         reads=[B.xab[dc]], writes=[B.tmpbb[tb1]])
                    K.op("act", lambda e, tb2=tb2, dc=dc: e.activation(out=B.tmpb[tb2][:, :], in_=B.xa[:, dc, :], func=AF.Square),
                         reads=[B.xab[dc]], writes=[B.tmpbb[tb2]])
                    pend.append((tb1, tb2, dc == 0, dc == KC - 1))
                W.release(1)
            while pend:
                tb1, tb2, first, last = pend.pop(0)
                K.op("pe", lambda e, tb1=tb1, first=first, last=last: e.matmul(
                    psum[4][:, :], lhsT=ones_bf[:, :], rhs=B.tmpb[tb1][:, :], start=first, stop=last),
                    reads=[B.tmpbb[tb1], b_const], writes=[psb[4]])
                K.op("pe", lambda e, tb2=tb2, first=first, last=last: e.matmul(
                    psum[5][:, :], lhsT=ones_bf[:, :], rhs=B.tmpb[tb2][:, :], start=first, stop=last),
                    reads=[B.tmpbb[tb2], b_const], writes=[psb[5]])
            ln_tail(B, V_LN2G, V_LN2B, V_LN2G, V_LN2B, True, True)
            ffn(B, W, (lambda: load_inputs(t + 1)) if t + 1 < NT else None)
            ln_tail(B, V_LN3G, V_LN3B, V_LN3G, V_LN3B, False, True, final_plain=True)
            for s in range(4):
                for q4 in range(4):
                    xi = sctr["xo"] % 2
                    sctr["xo"] += 1
                    pi = nps()
                    for j in range(4):
                        dc = q4 * 4 + j
                        K.op("pe", lambda e, pi=pi, j=j, dc=dc, s=s: e.transpose(
                            out=psum[pi][:, j * 128:(j + 1) * 128], in_=B.xa[:, dc, s * 128:(s + 1) * 128],
                            identity=ident[:, :]), reads=[B.xab[dc], b_const], writes=[psb[pi]])
                    if q4 % 2:
                        K.op("act", lambda e, pi=pi, xi=xi: e.activation(out=xout[xi][:, :], in_=psum[pi][:, :], func=AF.Copy),
                             reads=[psb[pi]], writes=[xoutb[xi]])
                    else:
                        K.op("dve", lambda e, pi=pi, xi=xi: e.tensor_copy(out=xout[xi][:, :], in_=psum[pi][:, :]),
                             reads=[psb[pi]], writes=[xoutb[xi]])
                    K.op("sp", lambda e, xi=xi, s=s, q4=q4: e.dma_start(
                        out=out_d[tok0 + s * 128:tok0 + (s + 1) * 128, q4 * 512:(q4 + 1) * 512], in_=xout[xi][:, :]),
                        reads=[xoutb[xi]], dma=1, key=("xout", xi))

        for t in range(NT):
            tile3(t)
        K.end_phase()

    es.close()
    return nc, dbg


def make_core_inputs(inputs):
    x = np.asarray(inputs["x"], dtype=np.float32)
    pos = np.asarray(inputs["positions"]).astype(np.int32)
    g = lambda k: np.asarray(inputs[k], dtype=np.float32)[0]

    vecs = np.zeros((128, NV), np.float32)

    def put(col, v, n):
        vecs[:, col:col + n] = v.reshape(n, 128).T

    put(V_LN1G, g("ln1_g"), 16)
    put(V_LN1B, g("ln1_b"), 16)
    put(V_LN2G, g("ln2_g"), 16)
    put(V_LN2B, g("ln2_b"), 16)
    put(V_LN3G, g("ln3_g"), 16)
    put(V_LN3B, g("ln3_b"), 16)
    put(V_QG, g("mla_q_norm_g"), 4)
    put(V_KVG, g("mla_kv_norm_g"), 4)
    put(V_SUBG, g("diff_subln_g"), 2)
    for i, k in enumerate(("diff_lambda_q1", "diff_lambda_k1", "diff_lambda_q2", "diff_lambda_k2")):
        put(V_LAM + i, g(k), 1)
    half = 32
    inv_freq = (10000.0 ** (-np.arange(half, dtype=np.float32) / half)).astype(np.float32)
    vecs[0:64, V_INVF] = np.concatenate([inv_freq, inv_freq])
    vecs[0:64, V_SGN] = np.concatenate([-np.ones(32, np.float32), np.ones(32, np.float32)])
    ident = np.eye(128, dtype=np.float32)

    wnames = ["ffn1_w_gate", "ffn1_w_up", "ffn1_w_down", "w_in", "mla_w_uq", "mla_w_uk", "mla_w_uv",
              "w_branch_mla", "w_branch_diff", "w_out", "ffn2_w_gate", "ffn2_w_up", "ffn2_w_down"]
    wts = {k: np.ascontiguousarray(g(k)) for k in wnames}
    maps = []
    tokidx = []
    for c in range(8):
        b, r = c // 2, c % 2
        ti = np.concatenate([np.arange(t * TT, (t + 1) * TT) for t in TILES[r]])
        tokidx.append(ti)
    for c in range(8):
        b, r = c // 2, c % 2
        ti = tokidx[c]
        kg = np.concatenate([tokidx[2 * b], tokidx[2 * b + 1]])
        m = dict(wts)
        m["x"] = np.ascontiguousarray(x[b, ti, :])
        m["qposb"] = np.ascontiguousarray(np.broadcast_to(pos[b, ti][None, :], (128, NTOK))).astype(np.int32)
        m["kposc"] = np.ascontiguousarray(pos[b, kg].reshape(32, 128).T).astype(np.int32)
        m["qidxb"] = np.ascontiguousarray(np.broadcast_to(ti[None, :].astype(np.float32), (128, NTOK)))
        m["kidxc"] = np.ascontiguousarray(kg.astype(np.float32).reshape(32, 128).T)
        m["vecs"] = vecs
        m["ident"] = ident
        maps.append(m)
    return maps, tokidx


_CACHE = {}


def kernel(**inputs):
    maps, tokidx = make_core_inputs(inputs)
    if "nc" not in _CACHE:
        _CACHE["nc"] = build_program(False)[0]
    nc = _CACHE["nc"]
    res = run_bass_kernel_spmd(nc, maps, core_ids=list(range(8)))
    out = np.empty((4, SEQ, D), np.float32)
    for c in range(8):
        out[c // 2, tokidx[c], :] = res.results[c]["out"]
    return out
```

```python
import math
import os
from contextlib import ExitStack

import numpy as np
import concourse.bass as bass
import concourse.mybir as mybir
from concourse.bass_utils import run_bass_kernel_spmd

F32 = mybir.dt.float32
BF16 = mybir.dt.bfloat16
I32 = mybir.dt.int32
ALU = mybir.AluOpType
AF = mybir.ActivationFunctionType

D = 2048
FF = 5632
SEQ = 4096
NTOK = 2048
TT = 512
NT = 4
KC = 16
FCH = 44
ALPHA = 2.0 ** 0.25
LN_EPS = 1e-5
RMS_EPS = 1e-6
LAMBDA_INIT = 0.8 - 0.6 * math.exp(-0.3 * 0)
SC_MLA = 192.0 ** -0.5
SC_DIFF = 128.0 ** -0.5
SLOPES = [2.0 ** (-8.0 * (h + 1) / 4) for h in range(4)]
TILES = ([0, 3, 4, 7], [1, 2, 5, 6])
SLOT_KEYS = [([], [0, 4]), ([0, 4], [5, 1]), ([0, 4, 5, 1], [2, 6]), ([0, 4, 5, 1, 2, 6], [7, 3])]
NSLOT_W = 4
EPOCH = 12000

C_CQ, C_CKV, C_KR, C_DQ, C_DK, C_DV, C_GM, C_GD = 0, 512, 1024, 1088, 2112, 3136, 4160, 6208

V_LN1G, V_LN1B, V_LN2G, V_LN2B, V_LN3G, V_LN3B = 0, 16, 32, 48, 64, 80
V_QG, V_KVG, V_SUBG, V_LAM, V_INVF, V_SGN = 96, 100, 104, 106, 110, 111
NV = 112


class Buf:
    __slots__ = ("name", "w", "r", "al", "excl")

    def __init__(self, name, excl=False):
        self.name = name
        self.excl = excl
        self.w = None
        self.r = []
        self.al = []


class Op:
    __slots__ = ("eng", "fn", "deps", "dma", "key", "sig", "cnt", "ep", "idx", "inc")


class Kern:
    ENGS = ("pe", "act", "dve", "pool", "sp")

    def __init__(self, nc, es):
        self.nc = nc
        self.es = es
        self.ops = []
        self.bufs = []
        self.engcnt = {e: 0 for e in self.ENGS}
        self.engep = {e: 0 for e in self.ENGS}
        self.engsem = {e: [] for e in self.ENGS}
        self.keysem = {}
        self.keycnt = {}
        self.lastkey = {}
        self.nsem = 0

    def buf(self, name, excl=False):
        b = Buf(name, excl)
        self.bufs.append(b)
        return b

    def sem(self, name):
        self.nsem += 1
        return self.es.enter_context(self.nc.semaphore(name))

    def op(self, eng, fn, reads=(), writes=(), dma=0, key=None, inc=16):
        o = Op()
        o.eng, o.fn, o.dma, o.key = eng, fn, dma, key
        o.inc = inc
        o.sig, o.cnt, o.ep = False, 0, 0
        deps = set()
        for b in reads:
            for bb in [b] + b.al:
                if bb.w is not None:
                    deps.add(bb.w)
                if bb.excl:
                    for r_ in bb.r:
                        if r_.eng != eng:
                            deps.add(r_)
        for b in writes:
            for bb in [b] + b.al:
                if bb.w is not None:
                    deps.add(bb.w)
                deps.update(bb.r)
        if dma:
            p = self.lastkey.get(key)
            if p is not None:
                deps.add(p)
            self.lastkey[key] = o
        for b in reads:
            b.r.append(o)
        for b in writes:
            b.w = o
            b.r = []
        deps.discard(o)
        o.deps = deps
        o.idx = len(self.ops)
        self.ops.append(o)
        return o

    def end_phase(self):
        last = {}
        dmas = []
        for o in self.ops:
            if o.dma:
                dmas.append(o)
            elif o.fn is not None:
                last[o.eng] = o
        lastd = {}
        for o in dmas:
            lastd[o.key] = o
        alld = set(last.values()) | set(lastd.values())
        for e in self.ENGS:
            o = Op()
            o.eng, o.fn, o.dma, o.key = e, None, 0, None
            o.inc = 16
            o.sig, o.cnt, o.ep = False, 0, 0
            o.deps = set(alld)
            o.idx = len(self.ops)
            self.ops.append(o)
        self._replay()
        self.ops = []
        self.lastkey = {}
        for b in self.bufs:
            b.w = None
            b.r = []

    def _replay(self):
        nc = self.nc
        ops = self.ops
        needed = set()
        for o in ops:
            for d in o.deps:
                if d.dma:
                    continue
                if d.eng == "pe" and o.eng == "pe":
                    continue
                needed.add(d)
        for o in ops:
            if o.dma:
                if o.key not in self.keysem:
                    self.keysem[o.key] = self.sem("k%d" % len(self.keysem))
                    self.keycnt[o.key] = 0
                self.keycnt[o.key] += o.inc * o.dma
                o.cnt = self.keycnt[o.key]
            elif o in needed:
                e = o.eng
                if self.engcnt[e] >= EPOCH or not self.engsem[e]:
                    self.engsem[e].append(self.sem("e%s%d" % (e, len(self.engsem[e]))))
                    self.engep[e] = len(self.engsem[e]) - 1
                    self.engcnt[e] = 0
                self.engcnt[e] += 1
                o.sig, o.cnt, o.ep = True, self.engcnt[e], self.engep[e]
        byeng = {e: [o for o in ops if o.eng == e] for e in self.ENGS}

        def run(e, name):
            seen_e = {}
            seen_k = {}
            for o in byeng[name]:
                we = {}
                wk = {}
                for d in o.deps:
                    if d.dma:
                        if wk.get(d.key, 0) < d.cnt:
                            wk[d.key] = d.cnt
                    else:
                        if d.eng == "pe" and name == "pe":
                            continue
                        v = (d.ep, d.cnt)
                        if we.get(d.eng, (-1, 0)) < v:
                            we[d.eng] = v
                for de, v in we.items():
                    if seen_e.get(de, (-1, 0)) >= v:
                        continue
                    seen_e[de] = v
                    e.wait_ge(self.engsem[de][v[0]], v[1])
                for k, c in wk.items():
                    if seen_k.get(k, 0) >= c:
                        continue
                    seen_k[k] = c
                    e.wait_ge(self.keysem[k], c)
                if o.fn is None:
                    continue
                ins = o.fn(e)
                if o.dma:
                    if not isinstance(ins, (list, tuple)):
                        ins = [ins]
                    assert len(ins) == o.dma
                    for i_ in ins:
                        i_.then_inc(self.keysem[o.key], o.inc)
                elif o.sig:
                    ins.then_inc(self.engsem[name][o.ep], 1)

        with nc.Block() as block:
            @block.tensor
            def _(e):
                run(e, "pe")

            @block.scalar
            def _(e):
                run(e, "act")

            @block.vector
            def _(e):
                run(e, "dve")

            @block.gpsimd
            def _(e):
                run(e, "pool")

            @block.sync
            def _(e):
                run(e, "sp")


class WStream:
    def __init__(self, K, slots, sbufs):
        self.K = K
        self.slots = slots
        self.sbufs = sbufs
        self.plan = []
        self.issued = 0
        self.taken = 0
        self.freed = 0

    def add(self, tag, srcs, shape):
        self.plan.append((tag, srcs, shape))

    def _issue(self, i):
        tag, srcs, (nk, ncols) = self.plan[i]
        s = i % len(self.slots)
        view = self.slots[s][:, 0:nk * ncols].rearrange("p (a b) -> p a b", a=nk)
        pairs = [(dst(view), src) for dst, src in srcs]

        def fn(e, pairs=pairs):
            return [e.dma_start(out=o_, in_=i_) for o_, i_ in pairs]

        self.K.op("pool", fn, writes=[self.sbufs[s]], dma=len(pairs), key=("wr", s))

    def _pump(self):
        while self.issued < len(self.plan) and self.issued - len(self.slots) < self.freed:
            self._issue(self.issued)
            self.issued += 1

    def next(self, tag):
        i = self.taken
        assert self.plan[i][0] == tag, (self.plan[i][0], tag)
        self._pump()
        assert self.issued > i, "weight ring deadlock: block %d (%s) not loadable" % (i, tag)
        self.taken += 1
        _, _, (nk, ncols) = self.plan[i]
        s = i % len(self.slots)
        view = self.slots[s][:, 0:nk * ncols].rearrange("p (a b) -> p a b", a=nk)
        return view, self.sbufs[s]

    def release(self, n=1):
        self.freed += n
        assert self.freed <= self.taken
        self._pump()


def _full(v):
    return v


def build_program(debug=False, stop_after=99, cut=99, ntile=NT, cut2=99):
    nc = bass.Bass("TRN2", target_bir_lowering=False)
    es = ExitStack()
    K = Kern(nc, es)

    def din(name, shape, dt=F32):
        return nc.dram_tensor(name, list(shape), dt, kind="ExternalInput").ap()

    x_d = din("x", [NTOK, D])
    qposb_d = din("qposb", [128, NTOK], I32)
    kposc_d = din("kposc", [128, 32], I32)
    qidxb_d = din("qidxb", [128, NTOK])
    kidxc_d = din("kidxc", [128, 32])
    vecs_d = din("vecs", [128, NV])
    ident_d = din("ident", [128, 128])
    w1g_d = din("ffn1_w_gate", [D, FF])
    w1u_d = din("ffn1_w_up", [D, FF])
    w1d_d = din("ffn1_w_down", [FF, D])
    win_d = din("w_in", [D, 8256])
    wuq_d = din("mla_w_uq", [512, 1536])
    wuk_d = din("mla_w_uk", [512, 1024])
    wuv_d = din("mla_w_uv", [512, 1024])
    wbm_d = din("w_branch_mla", [1024, D])
    wbd_d = din("w_branch_diff", [1024, D])
    wo_d = din("w_out", [D, D])
    w2g_d = din("ffn2_w_gate", [D, FF])
    w2u_d = din("ffn2_w_up", [D, FF])
    w2d_d = din("ffn2_w_down", [FF, D])
    out_d = nc.dram_tensor("out", [NTOK, D], F32, kind="ExternalOutput").ap()

    dbg = {}

    def dscr(name, shape, dt):
        if debug:
            t = nc.dram_tensor(name, list(shape), dt, kind="ExternalOutput")
            dbg[name] = t
            return t
        return nc.dram_tensor(name, list(shape), dt)

    H1A = dscr("H1A", [128, KC * NTOK], F32).ap().rearrange("p (c t) -> p c t", c=KC)
    H1B = dscr("H1B", [128, KC * NTOK], BF16).ap().rearrange("p (c t) -> p c t", c=KC)
    QN = dscr("QN", [128, 8 * NTOK], BF16).ap().rearrange("p (c t) -> p c t", c=8)
    QR = dscr("QR", [64, 8 * NTOK], BF16).ap().rearrange("p (c t) -> p c t", c=8)
    DQ = dscr("DQ", [128, 8 * NTOK], BF16).ap().rearrange("p (c t) -> p c t", c=8)
    OM = dscr("OM", [128, 8 * NTOK], BF16).ap().rearrange("p (c t) -> p c t", c=8)
    OD = dscr("OD", [128, 8 * NTOK], BF16).ap().rearrange("p (c t) -> p c t", c=8)
    KNX_t = [nc.dram_tensor("KNX%d" % i, [128, 4 * NTOK], BF16) for i in range(2)]
    DKX_t = [nc.dram_tensor("DKX%d" % i, [128, 4 * NTOK], BF16) for i in range(2)]
    VMX_t = [nc.dram_tensor("VMX%d" % i, [NTOK, 512], BF16) for i in range(2)]
    DVX_t = [nc.dram_tensor("DVX%d" % i, [NTOK, 512], BF16) for i in range(2)]
    KRX_t = nc.dram_tensor("KRX", [64, NTOK], BF16)
    KNG_t = [nc.dram_tensor("KNG%d" % i, [256, 4 * NTOK], BF16) for i in range(2)]
    DKG_t = [nc.dram_tensor("DKG%d" % i, [256, 4 * NTOK], BF16) for i in range(2)]
    VMG_t = [nc.dram_tensor("VMG%d" % i, [2 * NTOK, 512], BF16) for i in range(2)]
    DVG_t = [nc.dram_tensor("DVG%d" % i, [2 * NTOK, 512], BF16) for i in range(2)]
    KRG_t = nc.dram_tensor("KRG", [128, NTOK], BF16)
    KNX = [t_.ap().rearrange("p (c t) -> p c t", c=4) for t_ in KNX_t]
    DKX = [t_.ap().rearrange("p (c t) -> p c t", c=4) for t_ in DKX_t]
    VMX = [t_.ap() for t_ in VMX_t]
    DVX = [t_.ap() for t_ in DVX_t]
    KRX = KRX_t.ap()

    _nm = [0]

    def sb(name, shape, dt, stack=None):
        _nm[0] += 1
        return (stack or es).enter_context(nc.sbuf_tensor("s%d_%s" % (_nm[0], name), list(shape), dt))

    vecs = sb("vecs", [128, NV], F32)
    vecs2 = sb("vecs2", [128, 80], F32)
    ident = sb("ident", [128, 128], F32)
    ones_bf = sb("ones_bf", [128, 128], BF16)
    ones_f = sb("ones_f", [128, 128], F32)
    ident_bf = sb("ident_bf", [128, 128], BF16)
    lam = sb("lam", [128, 4], F32)
    b_const = K.buf("const")
    psum = [es.enter_context(nc.psum_tensor("ps%d" % i, [128, 512], F32)) for i in range(8)]
    psb = [K.buf("ps%d" % i, excl=True) for i in range(8)]

    def vcol(c, n=1, rows=128):
        return vecs[0:rows, c:c + n]

    K.op("sp", lambda e: [e.dma_start(out=vecs[:, :], in_=vecs_d[:, :]),
                          e.dma_start(out=ident[:, :], in_=ident_d[:, :])],
         writes=[b_const], dma=2, key="const")
    K.op("pool", lambda e: e.memset(ones_bf[:, :], 1.0), writes=[b_const])
    K.op("pool", lambda e: e.memset(ones_f[:, :], 1.0), writes=[b_const])
    K.op("dve", lambda e: e.tensor_copy(out=ident_bf[:, :], in_=ident[:, :]), reads=[b_const], writes=[b_const])
    K.op("dve", lambda e: e.tensor_scalar(out=vecs2[:, 0:64], in0=vecs[:, 0:64], scalar1=float(ALPHA),
                                          scalar2=None, op0=ALU.mult), reads=[b_const], writes=[b_const])
    K.op("dve", lambda e: e.tensor_tensor(out=vecs2[:, 64:65], in0=vcol(V_LAM), in1=vcol(V_LAM + 1), op=ALU.mult),
         reads=[b_const], writes=[b_const])
    K.op("dve", lambda e: e.tensor_tensor(out=vecs2[:, 65:66], in0=vcol(V_LAM + 2), in1=vcol(V_LAM + 3), op=ALU.mult),
         reads=[b_const], writes=[b_const])
    K.op("pe", lambda e: e.matmul(psum[0][:, 0:2], lhsT=ones_f[:, :], rhs=vecs2[:, 64:66], start=True, stop=True),
         reads=[b_const], writes=[psb[0]])
    K.op("act", lambda e: e.activation(out=vecs2[:, 66:68], in_=psum[0][:, 0:2], func=AF.Exp),
         reads=[psb[0]], writes=[b_const])
    K.op("dve", lambda e: e.tensor_tensor(out=lam[:, 0:1], in0=vecs2[:, 66:67], in1=vecs2[:, 67:68], op=ALU.subtract),
         reads=[b_const], writes=[b_const])
    K.op("dve", lambda e: e.tensor_scalar(out=lam[:, 0:1], in0=lam[:, 0:1], scalar1=float(LAMBDA_INIT), scalar2=None,
                                          op0=ALU.add), reads=[b_const], writes=[b_const])
    K.op("dve", lambda e: e.tensor_scalar(out=lam[:, 1:2], in0=lam[:, 0:1], scalar1=-1.0, scalar2=None,
                                          op0=ALU.mult), reads=[b_const], writes=[b_const])
    K.op("dve", lambda e: e.tensor_scalar(out=vecs2[:, 68:70], in0=vecs[:, V_SUBG:V_SUBG + 2],
                                          scalar1=float(1.0 - LAMBDA_INIT), scalar2=None, op0=ALU.mult),
         reads=[b_const], writes=[b_const])
    K.end_phase()
    if stop_after <= 0:
        es.close()
        return nc, dbg

    def wv(w, k0, k1, c0, c1):
        return w.rearrange("(kc p) n -> p kc n", p=128)[:, k0:k1, c0:c1]

    def ffn_plan(W, wg, wu, wd):
        for blk in range(11):
            W.add("g", [(_full, wv(wg, 0, 16, blk * 512, (blk + 1) * 512))], (16, 512))
            W.add("u", [(_full, wv(wu, 0, 16, blk * 512, (blk + 1) * 512))], (16, 512))
        for cb in range(8):
            W.add("d0", [(_full, wv(wd, 0, 32, cb * 256, (cb + 1) * 256))], (32, 256))
            W.add("d1", [(_full, wv(wd, 32, 44, cb * 256, (cb + 1) * 256))], (12, 256))

    class TileBufs:
        pass

    def alloc_tile_bufs(st):
        B = TileBufs()
        B.xb = sb("xb", [128, KC, TT], BF16, st)
        B.xa = sb("xa", [128, KC, TT], F32, st)
        B.act = sb("act", [128, FCH, TT], BF16, st)
        B.xbb = [K.buf("xb%d" % i) for i in range(KC)]
        B.xab = [K.buf("xa%d" % i) for i in range(KC)]
        B.actb = [K.buf("act%d" % i) for i in range(FCH)]
        B.tmpf = [sb("tmpf%d" % i, [128, TT], F32, st) for i in range(4)]
        B.tmpfb = [K.buf("tmpf%d" % i) for i in range(4)]
        B.tmpb = [sb("tmpb%d" % i, [128, TT], BF16, st) for i in range(4)]
        B.tmpbb = [K.buf("tmpb%d" % i) for i in range(4)]
        B.st = [sb("lnst%d" % i, [128, TT], F32, st) for i in range(3)]
        B.stb = [K.buf("lnst%d" % i) for i in range(3)]
        B.ctr = {"f": 0, "b": 0, "ps": 0}
        return B

    def ffn(B, W, mid_hook):
        for blk in range(11):
            wgv, wgb = W.next("g")
            wuv_, wub = W.next("u")
            for j in range(4):
                fc = blk * 4 + j
                pi = (fc % 4) * 2
                G, U = psum[pi], psum[pi + 1]
                for kc in range(KC):
                    K.op("pe", lambda e, G=G, kc=kc, j=j, wgv=wgv: e.matmul(
                        G[:, :], lhsT=wgv[:, kc, j * 128:(j + 1) * 128], rhs=B.xb[:, kc, :],
                        start=(kc == 0), stop=(kc == KC - 1)),
                        reads=[wgb, B.xbb[kc]], writes=[psb[pi]])
                for kc in range(KC):
                    K.op("pe", lambda e, U=U, kc=kc, j=j, wuv_=wuv_: e.matmul(
                        U[:, :], lhsT=wuv_[:, kc, j * 128:(j + 1) * 128], rhs=B.xb[:, kc, :],
                        start=(kc == 0), stop=(kc == KC - 1)),
                        reads=[wub, B.xbb[kc]], writes=[psb[pi + 1]])
                ti = B.ctr["f"] % 4
                B.ctr["f"] += 1
                tf, tfb = B.tmpf[ti], B.tmpfb[ti]
                K.op("act", lambda e, G=G, tf=tf: e.activation(out=tf[:, :], in_=G[:, :], func=AF.Silu),
                     reads=[psb[pi]], writes=[tfb])
                K.op("dve", lambda e, U=U, tf=tf, fc=fc: e.tensor_tensor(
                    out=B.act[:, fc, :], in0=tf[:, :], in1=U[:, :], op=ALU.mult),
                    reads=[tfb, psb[pi + 1]], writes=[B.actb[fc]])
            W.release(2)
        if mid_hook is not None:
            mid_hook()
        pend = []

        def stats(dc, first, last):
            tb1 = B.ctr["b"] % 4
            tb2 = (B.ctr["b"] + 1) % 4
            B.ctr["b"] += 2
            K.op("act", lambda e: e.activation(out=B.tmpb[tb1][:, :], in_=B.xa[:, dc, :], func=AF.Copy),
                 reads=[B.xab[dc]], writes=[B.tmpbb[tb1]])
            K.op("act", lambda e: e.activation(out=B.tmpb[tb2][:, :], in_=B.xa[:, dc, :], func=AF.Square),
                 reads=[B.xab[dc]], writes=[B.tmpbb[tb2]])
            pend.append((tb1, tb2, first, last))

        def flush_stats():
            while pend:
                tb1, tb2, first, last = pend.pop(0)
                K.op("pe", lambda e, tb1=tb1, first=first, last=last: e.matmul(
                    psum[4][:, :], lhsT=ones_bf[:, :], rhs=B.tmpb[tb1][:, :], start=first, stop=last),
                    reads=[B.tmpbb[tb1], b_const], writes=[psb[4]])
                K.op("pe", lambda e, tb2=tb2, first=first, last=last: e.matmul(
                    psum[5][:, :], lhsT=ones_bf[:, :], rhs=B.tmpb[tb2][:, :], start=first, stop=last),
                    reads=[B.tmpbb[tb2], b_const], writes=[psb[5]])

        for cb in range(8):
            w0, w0b = W.next("d0")
            w1, w1b = W.next("d1")
            base = (cb % 2) * 2
            for j in range(2):
                acc = psum[base + j]
                for kc in range(32):
                    K.op("pe", lambda e, acc=acc, kc=kc, j=j, w0=w0: e.matmul(
                        acc[:, :], lhsT=w0[:, kc, j * 128:(j + 1) * 128], rhs=B.act[:, kc, :],
                        start=(kc == 0), stop=False),
                        reads=[w0b, B.actb[kc]], writes=[psb[base + j]])
            for j in range(2):
                acc = psum[base + j]
                for kc in range(12):
                    K.op("pe", lambda e, acc=acc, kc=kc, j=j, w1=w1: e.matmul(
                        acc[:, :], lhsT=w1[:, kc, j * 128:(j + 1) * 128], rhs=B.act[:, 32 + kc, :],
                        start=False, stop=(kc == 11)),
                        reads=[w1b, B.actb[32 + kc]], writes=[psb[base + j]])
            W.release(2)
            flush_stats()
            for j in range(2):
                dc = cb * 2 + j
                acc = psum[base + j]
                K.op("dve", lambda e, acc=acc, dc=dc: e.scalar_tensor_tensor(
                    out=B.xa[:, dc, :], in0=acc[:, :], scalar=0.5, in1=B.xa[:, dc, :],
                    op0=ALU.mult, op1=ALU.add),
                    reads=[psb[base + j], B.xab[dc]], writes=[B.xab[dc]])
                stats(dc, dc == 0, dc == KC - 1)
        flush_stats()

    def ln_tail(B, gcol, bcol, ga, ba, want_b, want_a, final_plain=False):
        mu, var, rstd = B.st
        mub, varb, rstdb = B.stb
        K.op("dve", lambda e: e.tensor_scalar(out=mu[:, :], in0=psum[4][:, :], scalar1=1.0 / D, scalar2=None,
                                              op0=ALU.mult), reads=[psb[4]], writes=[mub])
        K.op("dve", lambda e: e.tensor_tensor(out=var[:, :], in0=mu[:, :], in1=mu[:, :], op=ALU.mult),
             reads=[mub], writes=[varb])
        K.op("dve", lambda e: e.scalar_tensor_tensor(out=var[:, :], in0=psum[5][:, :], scalar=1.0 / D, in1=var[:, :],
                                                     op0=ALU.mult, op1=ALU.subtract),
             reads=[psb[5], varb], writes=[varb])
        K.op("dve", lambda e: e.tensor_scalar(out=rstd[:, :], in0=var[:, :], scalar1=float(LN_EPS), scalar2=None,
                                              op0=ALU.add), reads=[varb], writes=[rstdb])
        K.op("act", lambda e: e.activation(out=rstd[:, :], in_=rstd[:, :], func=AF.Sqrt), reads=[rstdb], writes=[rstdb])
        K.op("dve", lambda e: e.reciprocal(out=rstd[:, :], in_=rstd[:, :]), reads=[rstdb], writes=[rstdb])
        for dc in range(KC):
            ti = B.ctr["f"] % 4
            B.ctr["f"] += 1
            tf, tfb = B.tmpf[ti], B.tmpfb[ti]
            K.op("dve", lambda e, tf=tf, dc=dc: e.tensor_tensor(out=tf[:, :], in0=B.xa[:, dc, :], in1=mu[:, :],
                                                                 op=ALU.subtract),
                 reads=[B.xab[dc], mub], writes=[tfb])
            K.op("dve", lambda e, tf=tf: e.tensor_tensor(out=tf[:, :], in0=tf[:, :], in1=rstd[:, :], op=ALU.mult),
                 reads=[tfb, rstdb], writes=[tfb])
            if want_b:
                K.op("act", lambda e, tf=tf, dc=dc: e.activation(
                    out=B.xb[:, dc, :], in_=tf[:, :], func=AF.Identity,
                    bias=vecs[:, bcol + dc:bcol + dc + 1], scale=vecs[:, gcol + dc:gcol + dc + 1]),
                    reads=[tfb, b_const], writes=[B.xbb[dc]])
            if want_a:
                src = vecs if final_plain else vecs2
                K.op("act", lambda e, tf=tf, dc=dc, src=src: e.activation(
                    out=B.xa[:, dc, :], in_=tf[:, :], func=AF.Identity,
                    bias=src[:, ba + dc:ba + dc + 1], scale=src[:, ga + dc:ga + dc + 1]),
                    reads=[tfb, b_const], writes=[B.xab[dc]])

    X_KN, X_VM, X_DK, X_DV, X_KR, X_H1B = [K.buf("x%d" % i) for i in range(6)]
    groups = [[0, 1], [2, 3], [4, 5], [6, 7]]
    pairs = [(KRX_t, KRG_t, X_KR)]
    for i_ in range(2):
        pairs += [(KNX_t[i_], KNG_t[i_], X_KN), (VMX_t[i_], VMG_t[i_], X_VM)]
    for i_ in range(2):
        pairs += [(DKX_t[i_], DKG_t[i_], X_DK), (DVX_t[i_], DVG_t[i_], X_DV)]
    gbufs = [K.buf("gath%d" % i) for i in range(len(pairs))]

    def issue_collectives():
        for i, (a, b, xb_) in enumerate(pairs):
            if cut2 == -1:
                break
            K.op("pool", lambda e, a=a, b=b: e.collective_compute(
                "AllGather", ALU.bypass, replica_groups=groups, ins=[a.ap().opt()], outs=[b.ap().opt()]),
                reads=[xb_], writes=[gbufs[i]], dma=1, key=("cc", i), inc=1)

    with ExitStack() as st:
        B = alloc_tile_bufs(st)
        wslots = [sb("wr%d" % i, [128, 8192], BF16, st) for i in range(NSLOT_W)]
        wsb = [K.buf("wr%d" % i) for i in range(NSLOT_W)]
        xin = [sb("xin%d" % i, [128, D // 2], F32, st) for i in range(2)]
        xinb = [K.buf("xin%d" % i) for i in range(2)]
        qpi = sb("qpi", [128, TT], I32, st)
        qpf = sb("qpf", [128, TT], F32, st)
        cosT = sb("cosT", [64, TT], F32, st)
        sinT = sb("sinT", [64, TT], F32, st)
        rang = sb("rang", [64, TT], F32, st)
        rtf = sb("rtf", [64, TT], F32, st)
        rti = sb("rti", [64, TT], I32, st)
        b_rope = K.buf("rope")
        cqn = sb("cqn", [128, 4, TT], BF16, st)
        cqnb = K.buf("cqn")
        stg = [B.act[:, 8 * i:8 * i + 8, :] for i in range(4)]
        stgb = [K.buf("stg%d" % i) for i in range(4)]
        for i in range(4):
            stgb[i].al = B.actb[8 * i:8 * i + 8]
        stgs = B.act[0:64, 32:34, :]
        stgsb = [K.buf("stgs0"), K.buf("stgs1")]
        stgsb[0].al = [B.actb[32]]
        stgsb[1].al = [B.actb[33]]
        for sb_ in stgb + stgsb:
            for ab_ in sb_.al:
                ab_.al.append(sb_)
        sctr = {"s": 0, "ss": 0, "ps": 0}

        W = WStream(K, wslots, wsb)
        for t in range(ntile):
            ffn_plan(W, w1g_d, w1u_d, w1d_d)
            W.add("ckv", [(_full, wv(win_d, 0, 16, C_CKV, C_CKV + 512))], (16, 512))
            for hb in range(2):
                W.add("dk", [(_full, wv(win_d, 0, 16, C_DK + hb * 512, C_DK + (hb + 1) * 512))], (16, 512))
            W.add("uk", [(_full, wv(wuk_d, 0, 4, 0, 1024))], (4, 1024))
            W.add("uv", [(_full, wv(wuv_d, 0, 4, 0, 1024))], (4, 1024))
            W.add("kr", [(lambda v: v[:, :, 0:64], wv(win_d, 0, 16, C_KR, C_KR + 64)),
                         (lambda v: v[:, :, 64:96], wv(win_d, 0, 16, C_KR + 32, C_KR + 64)),
                         (lambda v: v[:, :, 96:128], wv(win_d, 0, 16, C_KR, C_KR + 32))], (16, 128))
            for hb in range(2):
                W.add("dv", [(_full, wv(win_d, 0, 16, C_DV + hb * 512, C_DV + (hb + 1) * 512))], (16, 512))
            W.add("cq", [(_full, wv(win_d, 0, 16, C_CQ, C_CQ + 512))], (16, 512))
            for hb in range(2):
                W.add("dq", [(_full, wv(win_d, 0, 16, C_DQ + hb * 512, C_DQ + (hb + 1) * 512))], (16, 512))
            W.add("uq", [(_full, wv(wuq_d, 0, 4, 0, 1536))], (4, 1536))
            wuq_h = wuq_d.rearrange("(kc p) (h c) -> p kc h c", p=128, h=8)
            uqp_srcs = []
            for kc_ in range(4):
                uqp_srcs.append((lambda v, kc_=kc_: v[:, kc_, :].rearrange("p (h c) -> p h c", h=8)[:, :, 0:32],
                                 wuq_h[:, kc_, :, 160:192]))
                uqp_srcs.append((lambda v, kc_=kc_: v[:, kc_, :].rearrange("p (h c) -> p h c", h=8)[:, :, 32:64],
                                 wuq_h[:, kc_, :, 128:160]))
            W.add("uqp", uqp_srcs, (4, 512))

        def nps():
            i = sctr["ps"] % 8
            sctr["ps"] += 1
            return i

        def nstg():
            i = sctr["s"] % 4
            sctr["s"] += 1
            return i

        def evac_bf(eng, dst, src, reads, writes):
            if eng == "act":
                K.op("act", lambda e: e.activation(out=dst, in_=src, func=AF.Copy), reads=reads, writes=writes)
            else:
                K.op("dve", lambda e: e.tensor_copy(out=dst, in_=src), reads=reads, writes=writes)

        xpref = set()

        def tile1(t, part):
            tok0 = t * TT
            A = part == "A"
            if not A:
                K.op("sp", lambda e: e.dma_start(out=B.xb[:, :, :], in_=H1B[:, :, tok0:tok0 + TT]), reads=[X_H1B], writes=B.xbb,
                     dma=1, key="l_h1b")
            for s in range(4 if A else 0):
                for hf in range(2):
                    xi, xib = xin[hf], xinb[hf]
                    if (t, s, hf) not in xpref:
                        K.op("sp", lambda e, xi=xi, s=s, hf=hf: e.dma_start(
                            out=xi[:, :], in_=x_d[tok0 + s * 128:tok0 + (s + 1) * 128, hf * 1024:(hf + 1) * 1024]),
                            writes=[xib], dma=1, key=("xin", hf))
                    for q2 in range(2):
                        q4 = hf * 2 + q2
                        pi = nps()
                        for j in range(4):
                            K.op("pe", lambda e, pi=pi, j=j, q2=q2, xi=xi: e.transpose(
                                out=psum[pi][:, j * 128:(j + 1) * 128], in_=xi[:, (q2 * 4 + j) * 128:(q2 * 4 + j + 1) * 128],
                                identity=ident[:, :]), reads=[xib, b_const], writes=[psb[pi]])
                        pv = psum[pi][:, :].rearrange("p (a b) -> p a b", a=4)
                        K.op("act", lambda e, pv=pv, q4=q4, s=s: e.activation(
                            out=B.xa[:, q4 * 4:q4 * 4 + 4, s * 128:(s + 1) * 128], in_=pv, func=AF.Copy,
                            scale=float(ALPHA)), reads=[psb[pi]], writes=B.xab[q4 * 4:q4 * 4 + 4])
                        K.op("dve", lambda e, pv=pv, q4=q4, s=s: e.tensor_copy(
                            out=B.xb[:, q4 * 4:q4 * 4 + 4, s * 128:(s + 1) * 128], in_=pv),
                            reads=[psb[pi]], writes=B.xbb[q4 * 4:q4 * 4 + 4])
            K.op("sp", lambda e: e.dma_start(out=qpi[:, :], in_=qposb_d[:, tok0:tok0 + TT]),
                 writes=[b_rope], dma=1, key="qpi")
            K.op("dve", lambda e: e.tensor_copy(out=qpf[:, :], in_=qpi[:, :]), reads=[b_rope], writes=[b_rope])
            K.op("dve", lambda e: e.tensor_scalar(out=rang[:, :], in0=qpf[0:64, :], scalar1=vcol(V_INVF, 1, 64),
                                                  scalar2=None, op0=ALU.mult), reads=[b_rope, b_const], writes=[b_rope])
            for dstT, off in ((sinT, 0.0), (cosT, 0.5 * math.pi)):
                K.op("dve", lambda e, off=off: e.tensor_scalar(out=rtf[:, :], in0=rang[:, :], scalar1=float(off),
                                                               scalar2=float(1.0 / (2 * math.pi)), op0=ALU.add, op1=ALU.mult),
                     reads=[b_rope], writes=[b_rope])
                K.op("dve", lambda e: e.tensor_copy(out=rti[:, :], in_=rtf[:, :]), reads=[b_rope], writes=[b_rope])
                K.op("dve", lambda e: e.tensor_copy(out=rtf[:, :], in_=rti[:, :]), reads=[b_rope], writes=[b_rope])
                K.op("dve", lambda e: e.scalar_tensor_tensor(out=rtf[:, :], in0=rtf[:, :], scalar=float(-2 * math.pi),
                                                             in1=rang[:, :], op0=ALU.mult, op1=ALU.add),
                     reads=[b_rope], writes=[b_rope])
                K.op("dve", lambda e, off=off: e.tensor_scalar(out=rtf[:, :], in0=rtf[:, :], scalar1=float(off),
                                                               scalar2=float(-math.pi), op0=ALU.add, op1=ALU.max),
                     reads=[b_rope], writes=[b_rope])
                K.op("dve", lambda e: e.tensor_scalar(out=rtf[:, :], in0=rtf[:, :], scalar1=float(math.pi), scalar2=None,
                                                      op0=ALU.min), reads=[b_rope], writes=[b_rope])
                K.op("act", lambda e, dstT=dstT: e.activation(out=dstT[:, :], in_=rtf[:, :], func=AF.Sin),
                     reads=[b_rope], writes=[b_rope])
            K.op("dve", lambda e: e.tensor_scalar(out=sinT[:, :], in0=sinT[:, :], scalar1=vcol(V_SGN, 1, 64),
                                                  scalar2=None, op0=ALU.mult), reads=[b_rope, b_const], writes=[b_rope])
            if A:
                ffn(B, W, None)
            if debug and t == 0 and A:
                DZ = dscr("DBG_Z", [128, KC * TT], F32).ap().rearrange("p (c t) -> p c t", c=KC)
                DA = dscr("DBG_ACT", [128, FCH * TT], BF16).ap().rearrange("p (c t) -> p c t", c=FCH)
                DX = dscr("DBG_XB", [128, KC * TT], BF16).ap().rearrange("p (c t) -> p c t", c=KC)
                K.op("sp", lambda e: e.dma_start(out=DZ, in_=B.xa[:, :, :]), reads=B.xab, dma=1, key="dbgz")
                K.op("sp", lambda e: e.dma_start(out=DA, in_=B.act[:, :, :]), reads=B.actb, dma=1, key="dbga")
                K.op("sp", lambda e: e.dma_start(out=DX, in_=B.xb[:, :, :]), reads=B.xbb, dma=1, key="dbgx")
            if A:
                ln_tail(B, V_LN1G, V_LN1B, V_LN1G, V_LN1B, True, True)
                K.op("sp", lambda e: e.dma_start(out=H1B[:, :, tok0:tok0 + TT], in_=B.xb[:, :, :]),
                     reads=B.xbb, writes=[X_H1B], dma=1, key="h1b")
                K.op("sp", lambda e: e.dma_start(out=H1A[:, :, tok0:tok0 + TT], in_=B.xa[:, :, :]),
                     reads=B.xab, dma=1, key="h1a")
                if t + 1 < ntile:
                    for hf in range(2):
                        xpref.add((t + 1, 0, hf))
                        K.op("sp", lambda e, hf=hf: e.dma_start(
                            out=xin[hf][:, :], in_=x_d[tok0 + TT:tok0 + TT + 128, hf * 1024:(hf + 1) * 1024]),
                            writes=[xinb[hf]], dma=1, key=("xin", hf))

            def latent(tag, gcol):
                wv_, wb_ = W.next(tag)
                pis = []
                for c in range(4):
                    pi = nps()
                    pis.append(pi)
                    for kc in range(KC):
                        K.op("pe", lambda e, pi=pi, kc=kc, c=c: e.matmul(
                            psum[pi][:, :], lhsT=wv_[:, kc, c * 128:(c + 1) * 128], rhs=B.xb[:, kc, :],
                            start=(kc == 0), stop=(kc == KC - 1)), reads=[wb_, B.xbb[kc]], writes=[psb[pi]])
                    K.op("act", lambda e, pi=pi, c=c: e.activation(out=B.xa[:, c, :], in_=psum[pi][:, :], func=AF.Copy),
                         reads=[psb[pi]], writes=[B.xab[c]])
                    tb = B.ctr["b"] % 4
                    B.ctr["b"] += 1
                    K.op("act", lambda e, pi=pi, tb=tb: e.activation(out=B.tmpb[tb][:, :], in_=psum[pi][:, :], func=AF.Square),
                         reads=[psb[pi]], writes=[B.tmpbb[tb]])
                    pis.append(tb)
                W.release(1)
                ps_ss = nps()
                for c in range(4):
                    tb = pis[2 * c + 1]
                    K.op("pe", lambda e, tb=tb, c=c: e.matmul(psum[ps_ss][:, :], lhsT=ones_bf[:, :], rhs=B.tmpb[tb][:, :],
                                                             start=(c == 0), stop=(c == 3)),
                         reads=[B.tmpbb[tb], b_const], writes=[psb[ps_ss]])
                rinv, rinvb = B.st[0], B.stb[0]
                K.op("dve", lambda e: e.tensor_scalar(out=rinv[:, :], in0=psum[ps_ss][:, :], scalar1=1.0 / 512,
                                                      scalar2=float(RMS_EPS), op0=ALU.mult, op1=ALU.add),
                     reads=[psb[ps_ss]], writes=[rinvb])
                K.op("act", lambda e: e.activation(out=rinv[:, :], in_=rinv[:, :], func=AF.Sqrt), reads=[rinvb], writes=[rinvb])
                K.op("dve", lambda e: e.reciprocal(out=rinv[:, :], in_=rinv[:, :]), reads=[rinvb], writes=[rinvb])
                for c in range(4):
                    K.op("dve", lambda e, c=c: e.scalar_tensor_tensor(
                        out=cqn[:, c, :], in0=B.xa[:, c, :], scalar=vecs[:, gcol + c:gcol + c + 1], in1=rinv[:, :],
                        op0=ALU.mult, op1=ALU.mult), reads=[B.xab[c], rinvb, b_const], writes=[cqnb])

            def rope_out(pa, pb, dst, dstb):
                t1, t1b = B.tmpf[B.ctr["f"] % 4], B.tmpfb[B.ctr["f"] % 4]
                t2, t2b = B.tmpf[(B.ctr["f"] + 1) % 4], B.tmpfb[(B.ctr["f"] + 1) % 4]
                B.ctr["f"] += 2
                K.op("dve", lambda e: e.tensor_tensor(out=t1[0:64, :], in0=psum[pa][0:64, :], in1=cosT[:, :], op=ALU.mult),
                     reads=[psb[pa], b_rope], writes=[t1b])
                K.op("dve", lambda e: e.tensor_tensor(out=t2[0:64, :], in0=psum[pb][0:64, :], in1=sinT[:, :], op=ALU.mult),
                     reads=[psb[pb], b_rope], writes=[t2b])
                K.op("dve", lambda e: e.tensor_tensor(out=dst, in0=t1[0:64, :], in1=t2[0:64, :], op=ALU.add),
                     reads=[t1b, t2b], writes=[dstb])

            def q_path():
              wq, wqb = W.next("uq")
              wqp, wqpb = W.next("uqp")
              sq = nstg()
              for h in range(8):
                  pi = nps()
                  for kc in range(4):
                      K.op("pe", lambda e, pi=pi, kc=kc, h=h: e.matmul(
                          psum[pi][:, :], lhsT=wq[:, kc, h * 192:h * 192 + 128], rhs=cqn[:, kc, :],
                          start=(kc == 0), stop=(kc == 3)), reads=[wqb, cqnb], writes=[psb[pi]])
                  evac_bf("act" if h % 2 else "dve", stg[sq][:, h, :], psum[pi][:, :], [psb[pi]], [stgb[sq]])
              K.op("sp", lambda e, sq=sq: e.dma_start(out=QN[:, :, tok0:tok0 + TT], in_=stg[sq]),
                   reads=[stgb[sq]], dma=1, key="qn")
              for h in range(8):
                  pa, pb = nps(), nps()
                  for kc in range(4):
                      K.op("pe", lambda e, pa=pa, kc=kc, h=h: e.matmul(
                          psum[pa][0:64, :], lhsT=wq[:, kc, h * 192 + 128:h * 192 + 192], rhs=cqn[:, kc, :],
                          start=(kc == 0), stop=(kc == 3)), reads=[wqb, cqnb], writes=[psb[pa]])
                  for kc in range(4):
                      K.op("pe", lambda e, pb=pb, kc=kc, h=h: e.matmul(
                          psum[pb][0:64, :], lhsT=wqp[:, kc, h * 64:h * 64 + 64], rhs=cqn[:, kc, :],
                          start=(kc == 0), stop=(kc == 3)), reads=[wqpb, cqnb], writes=[psb[pb]])
                  ss = sctr["ss"] % 2
                  sctr["ss"] += 1
                  rope_out(pa, pb, stgs[:, ss, :], stgsb[ss])
                  K.op("sp", lambda e, ss=ss, h=h: e.dma_start(out=QR[:, h, tok0:tok0 + TT], in_=stgs[:, ss, :]),
                       reads=[stgsb[ss]], dma=1, key=("qr", ss))
              W.release(2)

            def kv_path():
              wk, wkb = W.next("uk")
              sk = nstg()
              for h in range(8):
                  pi = nps()
                  for kc in range(4):
                      K.op("pe", lambda e, pi=pi, kc=kc, h=h: e.matmul(
                          psum[pi][:, :], lhsT=wk[:, kc, h * 128:(h + 1) * 128], rhs=cqn[:, kc, :],
                          start=(kc == 0), stop=(kc == 3)), reads=[wkb, cqnb], writes=[psb[pi]])
                  evac_bf("act" if h % 2 else "dve", stg[sk][:, h, :], psum[pi][:, :], [psb[pi]], [stgb[sk]])
              K.op("sp", lambda e, sk=sk: [e.dma_start(out=KNX[i_][:, :, tok0:tok0 + TT], in_=stg[sk][:, 4 * i_:4 * i_ + 4, :])
                                           for i_ in range(2)], reads=[stgb[sk]], writes=[X_KN], dma=2, key="knx")
              W.release(1)
              wvv, wvb = W.next("uv")
              sv = nstg()
              svv = stg[sv].rearrange("p (s a) t -> p s (a t)", s=4)
              for s in range(4):
                  for hb in range(2):
                      pi = nps()
                      for kc in range(4):
                          K.op("pe", lambda e, pi=pi, kc=kc, s=s, hb=hb: e.matmul(
                              psum[pi][:, :], lhsT=cqn[:, kc, s * 128:(s + 1) * 128], rhs=wvv[:, kc, hb * 512:(hb + 1) * 512],
                              start=(kc == 0), stop=(kc == 3)), reads=[wvb, cqnb], writes=[psb[pi]])
                      evac_bf("act" if hb else "dve", svv[:, s, hb * 512:(hb + 1) * 512], psum[pi][:, :],
                              [psb[pi]], [stgb[sv]])
              K.op("sp", lambda e, svv=svv: [e.dma_start(
                  out=VMX[i_][tok0:tok0 + TT, :].rearrange("(s p) n -> p s n", p=128), in_=svv[:, :, i_ * 512:(i_ + 1) * 512])
                  for i_ in range(2)], reads=[stgb[sv]], writes=[X_VM], dma=2, key="vmx")
              W.release(1)
              wkr, wkrb = W.next("kr")
              pa, pb = nps(), nps()
              for kc in range(KC):
                  K.op("pe", lambda e, pa=pa, kc=kc: e.matmul(psum[pa][0:64, :], lhsT=wkr[:, kc, 0:64], rhs=B.xb[:, kc, :],
                                                              start=(kc == 0), stop=(kc == KC - 1)),
                       reads=[wkrb, B.xbb[kc]], writes=[psb[pa]])
              for kc in range(KC):
                  K.op("pe", lambda e, pb=pb, kc=kc: e.matmul(psum[pb][0:64, :], lhsT=wkr[:, kc, 64:128], rhs=B.xb[:, kc, :],
                                                              start=(kc == 0), stop=(kc == KC - 1)),
                       reads=[wkrb, B.xbb[kc]], writes=[psb[pb]])
              W.release(1)
              ss = sctr["ss"] % 2
              sctr["ss"] += 1
              rope_out(pa, pb, stgs[:, ss, :], stgsb[ss])
              K.op("sp", lambda e, ss=ss: e.dma_start(out=KRX[:, tok0:tok0 + TT], in_=stgs[:, ss, :]),
                   reads=[stgsb[ss]], writes=[X_KR], dma=1, key=("qr", ss))
            def dqk(nm, dst, key):
                sd = nstg()
                for hb in range(2):
                    wd_, wdb_ = W.next(nm)
                    for c in range(4):
                        pi = nps()
                        for kc in range(KC):
                            K.op("pe", lambda e, pi=pi, kc=kc, c=c, wd_=wd_: e.matmul(
                                psum[pi][:, :], lhsT=wd_[:, kc, c * 128:(c + 1) * 128], rhs=B.xb[:, kc, :],
                                start=(kc == 0), stop=(kc == KC - 1)), reads=[wdb_, B.xbb[kc]], writes=[psb[pi]])
                        evac_bf("act" if c % 2 else "dve", stg[sd][:, hb * 4 + c, :], psum[pi][:, :], [psb[pi]], [stgb[sd]])
                    W.release(1)
                if dst is not None:
                    K.op("sp", lambda e, sd=sd, dst=dst: e.dma_start(out=dst[:, :, tok0:tok0 + TT], in_=stg[sd]),
                         reads=[stgb[sd]], dma=1, key=key)
                else:
                    K.op("sp", lambda e, sd=sd: [e.dma_start(out=DKX[i_][:, :, tok0:tok0 + TT], in_=stg[sd][:, 4 * i_:4 * i_ + 4, :])
                                                 for i_ in range(2)], reads=[stgb[sd]], writes=[X_DK], dma=2, key=key)
            latent("ckv", V_KVG)
            dqk("dk", None, "dkx")
            kv_path()
            sv = nstg()
            svv = stg[sv].rearrange("p (s a) t -> p s (a t)", s=4)
            for hb in range(2 if A else 0):
                wd_, wdb_ = W.next("dv")
                for s in range(4):
                    pi = nps()
                    for kc in range(KC):
                        K.op("pe", lambda e, pi=pi, kc=kc, s=s, wd_=wd_: e.matmul(
                            psum[pi][:, :], lhsT=B.xb[:, kc, s * 128:(s + 1) * 128], rhs=wd_[:, kc, :],
                            start=(kc == 0), stop=(kc == KC - 1)), reads=[wdb_, B.xbb[kc]], writes=[psb[pi]])
                    evac_bf("act" if s % 2 else "dve", svv[:, s, hb * 512:(hb + 1) * 512], psum[pi][:, :],
                            [psb[pi]], [stgb[sv]])
                W.release(1)
            if A:
                K.op("sp", lambda e, svv=svv: [e.dma_start(
                    out=DVX[i_][tok0:tok0 + TT, :].rearrange("(s p) n -> p s n", p=128), in_=svv[:, :, i_ * 512:(i_ + 1) * 512])
                    for i_ in range(2)], reads=[stgb[sv]], writes=[X_DV], dma=2, key="dvx")
            latent("cq", V_QG)
            dqk("dq", DQ, "dqx")
            q_path()

        for t in range(ntile):
            tile1(t, "A")
        K.end_phase()
    if stop_after <= 1:
        es.close()
        return nc, dbg

    G_KR = gbufs[0]
    G_KN = [gbufs[1], gbufs[3]]
    G_VM = [gbufs[2], gbufs[4]]
    G_DK = [gbufs[5], gbufs[7]]
    G_DV = [gbufs[6], gbufs[8]]
    KNG = [t_.ap().rearrange("(r p) (c t) -> r p c t", r=2, c=4) for t_ in KNG_t]
    DKG = [t_.ap().rearrange("(r p) (c t) -> r p c t", r=2, c=4) for t_ in DKG_t]
    KRG = KRG_t.ap().rearrange("(r p) t -> r p t", r=2)
    VMG = [t_.ap().rearrange("(c p) n -> p c n", p=128) for t_ in VMG_t]
    DVG = [t_.ap().rearrange("(c p) n -> p c n", p=128) for t_ in DVG_t]

    issue_collectives()
    with ExitStack() as st:
        kb = [sb("kb%d" % i, [128, 2 * NTOK], BF16, st) for i in range(2)]
        kbb = [K.buf("kb%d" % i) for i in range(2)]
        vb = [sb("vb%d" % i, [128, 32, 256], BF16, st) for i in range(2)]
        vbb = [K.buf("vb%d" % i) for i in range(2)]
        qb = [sb("qb%d" % i, [128, NTOK], BF16, st) for i in range(2)]
        qbb = [K.buf("qb%d" % i) for i in range(2)]
        qrb = [sb("qrb%d" % i, [64, NTOK], BF16, st) for i in range(2)]
        qrbb = [K.buf("qrb%d" % i) for i in range(2)]
        krb = sb("krb", [64, 2 * NTOK], BF16, st)
        krbb = K.buf("krb")
        pen = sb("pen", [128, 32, TT], BF16, st)
        penb = K.buf("pen")
        qidx = sb("qidx", [128, NTOK], F32, st)
        kidx = sb("kidx", [128, 32], F32, st)
        qpi2 = sb("qpi2", [128, NTOK], I32, st)
        kpi2 = sb("kpi2", [128, 32], I32, st)
        qpos = sb("qpos", [128, NTOK], F32, st)
        kpos = sb("kpos", [128, 32], F32, st)
        b_idx = K.buf("idx")
        NDR = 6
        dring = [sb("dist%d" % i, [128, TT], F32, st) for i in range(NDR)]
        dringb = [K.buf("dist%d" % i) for i in range(NDR)]
        NSR = 4
        sring = [sb("sfix%d" % i, [128, TT], F32, st) for i in range(NSR)]
        sringb = [K.buf("sfix%d" % i) for i in range(NSR)]
        NPR = 4
        pring = [sb("pT%d" % i, [128, TT], BF16, st) for i in range(NPR)]
        pringb = [K.buf("pT%d" % i) for i in range(NPR)]
        fin = [sb("fin%d" % i, [128, TT], F32, st) for i in range(6)]
        finb = [K.buf("fin%d" % i) for i in range(6)]
        o1 = sb("o1", [128, 2, TT], F32, st)
        o1b = K.buf("o1")
        od = sb("od", [128, 2, TT], F32, st)
        odb = K.buf("od")
        ost = [sb("ost%d" % i, [128, 2, TT], BF16, st) for i in range(2)]
        ostb = [K.buf("ost%d" % i) for i in range(2)]
        ptmp = [sb("ptmp%d" % i, [128, TT], F32, st) for i in range(2)]
        ptmpb = [K.buf("ptmp%d" % i) for i in range(2)]
        ctr = {"d": 0, "s": 0, "p": 0, "o": 0, "S": 0, "acc": 0, "pt": 0}

        K.op("sp", lambda e: [e.dma_start(out=qidx[:, :], in_=qidxb_d[:, :]),
                              e.dma_start(out=kidx[:, :], in_=kidxc_d[:, :]),
                              e.dma_start(out=qpi2[:, :], in_=qposb_d[:, :]),
                              e.dma_start(out=kpi2[:, :], in_=kposc_d[:, :])],
             writes=[b_idx], dma=4, key="idx")
        K.op("dve", lambda e: e.tensor_copy(out=qpos[:, :], in_=qpi2[:, :]), reads=[b_idx], writes=[b_idx])
        K.op("dve", lambda e: e.tensor_copy(out=kpos[:, :], in_=kpi2[:, :]), reads=[b_idx], writes=[b_idx])
        K.op("dve", lambda e: e.tensor_scalar(out=kpos[:, :], in0=kpos[:, :], scalar1=-1.0, scalar2=None, op0=ALU.mult),
             reads=[b_idx], writes=[b_idx])
        pen_of = {}
        for j in range(4):
            for gi, g_ in enumerate(SLOT_KEYS[j][1]):
                for i in range(4):
                    c = g_ * 4 + i
                    pidx = j * 8 + gi * 4 + i
                    pen_of[(j, c)] = pidx
                    K.op("dve", lambda e, pidx=pidx, j=j, c=c: e.tensor_scalar(
                        out=pen[:, pidx, :], in0=qidx[:, j * TT:(j + 1) * TT], scalar1=kidx[:, c:c + 1],
                        scalar2=-30000.0, op0=ALU.is_lt, op1=ALU.mult), reads=[b_idx], writes=[penb])
        K.op("sp", lambda e: [e.dma_start(out=krb[:, r * NTOK:(r + 1) * NTOK], in_=KRG[r]) for r in range(2)],
             reads=[G_KR], writes=[krbb], dma=2, key="krb")

        def chunks_of(j):
            un, ma = SLOT_KEYS[j]
            return [(g_ * 4 + i, False) for g_ in un for i in range(4)] + \
                   [(g_ * 4 + i, True) for g_ in ma for i in range(4)]

        MLA_ACC = [(4, 5), (6, 7)]

        def mla_group(h, s, j, accO, accZ):
            tiles = chunks_of(j)
            n = len(tiles)
            sbank = {}
            qn_ = qb[s][:, j * TT:(j + 1) * TT]
            qr_ = qrb[s][:, j * TT:(j + 1) * TT]

            def emitS(i):
                c = tiles[i][0]
                bi = ctr["S"] % 4
                ctr["S"] += 1
                sbank[i] = bi
                K.op("pe", lambda e: e.matmul(psum[bi][:, :], lhsT=kb[s][:, c * 128:(c + 1) * 128], rhs=qn_, start=True, stop=False),
                     reads=[kbb[s], qbb[s]], writes=[psb[bi]])
                masked = tiles[i][1]
                K.op("pe", lambda e: e.matmul(psum[bi][:, :], lhsT=krb[:, c * 128:(c + 1) * 128], rhs=qr_, start=False,
                                              stop=(not masked)),
                     reads=[krbb, qrbb[s]], writes=[psb[bi]])
                if masked:
                    pidx = pen_of[(j, c)]
                    K.op("pe", lambda e: e.matmul(psum[bi][:, :], lhsT=ident_bf[:, :], rhs=pen[:, pidx, :], start=False, stop=True),
                         reads=[b_const, penb], writes=[psb[bi]])

            def emitP(i):
                c, masked = tiles[i]
                bi = sbank[i]
                pi = ctr["p"] % NPR
                ctr["p"] += 1
                src, srcb = psum[bi][:, :], [psb[bi]]
                K.op("act", lambda e: e.activation(out=pring[pi][:, :], in_=src, func=AF.Exp, scale=float(SC_MLA)),
                     reads=srcb, writes=[pringb[pi]])
                return pi

            def emitPV(i, pi):
                c = tiles[i][0]
                K.op("pe", lambda e: e.matmul(psum[accO][:, :], lhsT=vb[s][:, c, 0:128], rhs=pring[pi][:, :],
                                              start=(i == 0), stop=(i == n - 1)),
                     reads=[vbb[s], pringb[pi]], writes=[psb[accO]])
                K.op("pe", lambda e: e.matmul(psum[accZ][:, :], lhsT=ones_bf[:, :], rhs=pring[pi][:, :],
                                              start=(i == 0), stop=(i == n - 1)),
                     reads=[b_const, pringb[pi]], writes=[psb[accZ]])

            DEPTH = 3
            for i in range(min(DEPTH, n)):
                emitS(i)
            for i in range(n):
                pi = emitP(i)
                emitPV(i, pi)
                if i + DEPTH < n:
                    emitS(i + DEPTH)

        def load_mla_head(h):
            s = h % 2
            K.op("sp", lambda e: [e.dma_start(out=kb[s][:, r * NTOK:(r + 1) * NTOK], in_=KNG[h // 4][r][:, h % 4, :]) for r in range(2)],
                 reads=[G_KN[h // 4]], writes=[kbb[s]], dma=2, key=("kb", s))
            K.op("sp", lambda e: e.dma_start(out=vb[s][:, :, 0:128], in_=VMG[h // 4][:, :, (h % 4) * 128:(h % 4 + 1) * 128]),
                 reads=[G_VM[h // 4]], writes=[vbb[s]], dma=1, key=("vb", s))
            K.op("sp", lambda e: e.dma_start(out=qb[s][:, :], in_=QN[:, h, :]), writes=[qbb[s]], dma=1, key=("qb", s))
            K.op("sp", lambda e: e.dma_start(out=qrb[s][:, :], in_=QR[:, h, :]), writes=[qrbb[s]], dma=1, key=("qrb", s))

        if cut2 >= 2:
            load_mla_head(0)
        for h in range(8 if cut2 >= 3 else (1 if cut2 >= 2 else 0)):
            if h + 1 < 8:
                load_mla_head(h + 1)
            s = h % 2
            for j in range(4):
                accO, accZ = MLA_ACC[ctr["acc"] % 2]
                ctr["acc"] += 1
                mla_group(h, s, j, accO, accZ)
                fi = ctr["o"] % 6
                ctr["o"] += 1
                oi = ctr["o"] % 2
                K.op("dve", lambda e, fi=fi, accZ=accZ: e.reciprocal(out=fin[fi][:, :], in_=psum[accZ][:, :]),
                     reads=[psb[accZ]], writes=[finb[fi]])
                K.op("dve", lambda e, fi=fi, oi=oi, accO=accO: e.tensor_tensor(
                    out=ost[oi][:, 0, :], in0=psum[accO][:, :], in1=fin[fi][:, :], op=ALU.mult),
                    reads=[psb[accO], finb[fi]], writes=[ostb[oi]])
                K.op("sp", lambda e, oi=oi, h=h, j=j: e.dma_start(out=OM[:, h, j * TT:(j + 1) * TT], in_=ost[oi][:, 0, :]),
                     reads=[ostb[oi]], dma=1, key=("ost", oi))

        DACC = [(2, 3, 4), (5, 6, 7)]

        def load_diff_head(h):
            K.op("sp", lambda e: e.dma_start(out=vb[h % 2][:, :, :], in_=DVG[h // 2][:, :, (h % 2) * 256:(h % 2 + 1) * 256]),
                 reads=[G_DV[h // 2]], writes=[vbb[h % 2]], dma=1, key=("vb", h % 2))
            for m in range(2):
                hm = 2 * h + m
                K.op("sp", lambda e, m=m, hm=hm: [e.dma_start(out=kb[m][:, r * NTOK:(r + 1) * NTOK], in_=DKG[hm // 4][r][:, hm % 4, :])
                                                 for r in range(2)],
                     reads=[G_DK[hm // 4]], writes=[kbb[m]], dma=2, key=("kb", m))
                K.op("sp", lambda e, m=m, hm=hm: e.dma_start(out=qb[m][:, :], in_=DQ[:, hm, :]), writes=[qbb[m]], dma=1, key=("qb", m))

        def diff_group(h, j):
            tiles = chunks_of(j)
            n = len(tiles)
            cneg = -SLOPES[h] / SC_DIFF
            vs = h % 2
            dtile = {}

            def emitS(i):
                c = tiles[i][0]
                masked = tiles[i][1]
                for m in range(2):
                    K.op("pe", lambda e, m=m: e.matmul(psum[m][:, :], lhsT=kb[m][:, c * 128:(c + 1) * 128],
                                                       rhs=qb[m][:, j * TT:(j + 1) * TT], start=True, stop=(not masked)),
                         reads=[kbb[m], qbb[m]], writes=[psb[m]])
                    if masked:
                        pidx = pen_of[(j, c)]
                        K.op("pe", lambda e, m=m, pidx=pidx: e.matmul(psum[m][:, :], lhsT=ident_bf[:, :], rhs=pen[:, pidx, :],
                                                                     start=False, stop=True),
                             reads=[b_const, penb], writes=[psb[m]])

            def emitD(i):
                c = tiles[i][0]
                di = ctr["d"] % NDR
                ctr["d"] += 1
                dtile[i] = di
                K.op("act", lambda e: e.activation(out=dring[di][:, :], in_=qpos[:, j * TT:(j + 1) * TT], func=AF.Abs,
                                                   bias=kpos[:, c:c + 1], scale=1.0), reads=[b_idx], writes=[dringb[di]])

            def emitP(i):
                c, masked = tiles[i]
                di = dtile[i]
                pis = []
                for m in range(2):
                    si = ctr["s"] % NSR
                    ctr["s"] += 1
                    K.op("dve", lambda e, m=m, si=si: e.scalar_tensor_tensor(
                        out=sring[si][:, :], in0=dring[di][:, :], scalar=float(cneg), in1=psum[m][:, :],
                        op0=ALU.mult, op1=ALU.add), reads=[dringb[di], psb[m]], writes=[sringb[si]])
                    pi = ctr["p"] % NPR
                    ctr["p"] += 1
                    K.op("act", lambda e, si=si, pi=pi: e.activation(out=pring[pi][:, :], in_=sring[si][:, :], func=AF.Exp,
                                                                    scale=float(SC_DIFF)),
                         reads=[sringb[si]], writes=[pringb[pi]])
                    pis.append(pi)
                return pis

            def emitPV(i, pis):
                c = tiles[i][0]
                for m in range(2):
                    a0, a1, az = DACC[m]
                    for d_, bank in ((0, a0), (1, a1)):
                        K.op("pe", lambda e, m=m, d_=d_, bank=bank: e.matmul(
                            psum[bank][:, :], lhsT=vb[vs][:, c, d_ * 128:(d_ + 1) * 128], rhs=pring[pis[m]][:, :],
                            start=(i == 0), stop=(i == n - 1)), reads=[vbb[vs], pringb[pis[m]]], writes=[psb[bank]])
                    K.op("pe", lambda e, m=m, az=az: e.matmul(psum[az][:, :], lhsT=ones_bf[:, :], rhs=pring[pis[m]][:, :],
                                                             start=(i == 0), stop=(i == n - 1)),
                         reads=[b_const, pringb[pis[m]]], writes=[psb[az]])

            emitS(0)
            emitD(0)
            for i in range(n):
                if i + 1 < n:
                    emitD(i + 1)
                pis = emitP(i)
                if i == min(2, n - 1) and pending_fin:
                    pending_fin.pop(0)()
                if i + 1 < n:
                    emitS(i + 1)
                emitPV(i, pis)

            f1 = ctr["o"] % 6
            f2 = (ctr["o"] + 1) % 6
            f3 = (ctr["o"] + 2) % 6
            ctr["o"] += 3
            K.op("dve", lambda e: e.tensor_copy(out=fin[f1][:, :], in_=psum[DACC[0][2]][:, :]),
                 reads=[psb[DACC[0][2]]], writes=[finb[f1]])
            K.op("act", lambda e: e.activation(out=fin[f2][:, :], in_=psum[DACC[1][2]][:, :], func=AF.Copy),
                 reads=[psb[DACC[1][2]]], writes=[finb[f2]])
            K.op("dve", lambda e: e.tensor_copy(out=o1[:, 0, :], in_=psum[DACC[0][0]][:, :]),
                 reads=[psb[DACC[0][0]]], writes=[o1b])
            K.op("act", lambda e: e.activation(out=o1[:, 1, :], in_=psum[DACC[0][1]][:, :], func=AF.Copy),
                 reads=[psb[DACC[0][1]]], writes=[o1b])
            K.op("dve", lambda e: e.tensor_copy(out=od[:, 0, :], in_=psum[DACC[1][0]][:, :]),
                 reads=[psb[DACC[1][0]]], writes=[odb])
            K.op("act", lambda e: e.activation(out=od[:, 1, :], in_=psum[DACC[1][1]][:, :], func=AF.Copy),
                 reads=[psb[DACC[1][1]]], writes=[odb])
            pending_fin.append(lambda: fin_part2(h, j, f1, f2, f3))

        def fin_part2(h, j, f1, f2, f3):
            K.op("dve", lambda e: e.reciprocal(out=fin[f1][:, :], in_=fin[f1][:, :]), reads=[finb[f1]], writes=[finb[f1]])
            K.op("dve", lambda e: e.reciprocal(out=fin[f2][:, :], in_=fin[f2][:, :]), reads=[finb[f2]], writes=[finb[f2]])
            K.op("dve", lambda e: e.tensor_scalar(out=fin[f2][:, :], in0=fin[f2][:, :], scalar1=lam[:, 1:2], scalar2=None,
                                                  op0=ALU.mult), reads=[finb[f2], b_const], writes=[finb[f2]])
            for d_ in range(2):
                K.op("dve", lambda e, d_=d_: e.tensor_tensor(out=o1[:, d_, :], in0=o1[:, d_, :], in1=fin[f1][:, :],
                                                            op=ALU.mult), reads=[o1b, finb[f1]], writes=[o1b])
                K.op("dve", lambda e, d_=d_: e.tensor_tensor(out=od[:, d_, :], in0=od[:, d_, :], in1=fin[f2][:, :],
                                                            op=ALU.mult), reads=[odb, finb[f2]], writes=[odb])
            for d_ in range(2):
                K.op("dve", lambda e, d_=d_: e.tensor_tensor(out=od[:, d_, :], in0=od[:, d_, :], in1=o1[:, d_, :], op=ALU.add),
                     reads=[odb, o1b], writes=[odb])
            tbs = []
            for d_ in range(2):
                pi = ctr["p"] % NPR
                ctr["p"] += 1
                tbs.append(pi)
                K.op("act", lambda e, pi=pi, d_=d_: e.activation(out=pring[pi][:, :], in_=od[:, d_, :], func=AF.Square),
                     reads=[odb], writes=[pringb[pi]])
            for d_ in range(2):
                K.op("pe", lambda e, d_=d_, pi=tbs[d_]: e.matmul(
                    psum[0][:, :], lhsT=ones_bf[:, :], rhs=pring[pi][:, :], start=(d_ == 0), stop=(d_ == 1)),
                    reads=[pringb[tbs[d_]], b_const], writes=[psb[0]])
            K.op("dve", lambda e: e.tensor_scalar(out=fin[f3][:, :], in0=psum[0][:, :], scalar1=1.0 / 256, scalar2=float(RMS_EPS),
                                                  op0=ALU.mult, op1=ALU.add), reads=[psb[0]], writes=[finb[f3]])
            K.op("act", lambda e: e.activation(out=fin[f3][:, :], in_=fin[f3][:, :], func=AF.Sqrt), reads=[finb[f3]], writes=[finb[f3]])
            K.op("dve", lambda e: e.reciprocal(out=fin[f3][:, :], in_=fin[f3][:, :]), reads=[finb[f3]], writes=[finb[f3]])
            oi = ctr["o"] % 2
            for d_ in range(2):
                K.op("dve", lambda e, d_=d_: e.scalar_tensor_tensor(
                    out=ost[oi][:, d_, :], in0=od[:, d_, :], scalar=vecs2[:, 68 + d_:69 + d_], in1=fin[f3][:, :],
                    op0=ALU.mult, op1=ALU.mult), reads=[odb, finb[f3], b_const], writes=[ostb[oi]])
            K.op("sp", lambda e: e.dma_start(out=OD[:, 2 * h:2 * h + 2, j * TT:(j + 1) * TT], in_=ost[oi][:, :, :]),
                 reads=[ostb[oi]], dma=1, key=("ost", oi))

        pending_fin = []
        for h in range(4 if cut2 >= 4 else 0):
            load_diff_head(h)
            for j in range(4):
                diff_group(h, j)
        while pending_fin:
            pending_fin.pop(0)()
        K.end_phase()

    if stop_after <= 2:
        es.close()
        return nc, dbg

    with ExitStack() as st:
        B = alloc_tile_bufs(st)
        wslots = [sb("wr%d" % i, [128, 8192], BF16, st) for i in range(NSLOT_W)]
        wsb = [K.buf("wr%d" % i) for i in range(NSLOT_W)]
        xout = [sb("xout%d" % i, [128, 512], F32, st) for i in range(2)]
        xoutb = [K.buf("xout%d" % i) for i in range(2)]
        om = sb("om", [128, 8, TT], BF16, st)
        omb = K.buf("om")
        odt = sb("odt", [128, 8, TT], BF16, st)
        odtb = K.buf("odt")
        ytmp = sb("ytmp", [128, 4, TT], F32, st)
        ytmpb = [K.buf("ytmp%d" % i) for i in range(4)]
        yT = B.act[:, 0:16, :]
        yb = B.actb[0:16]
        sctr = {"ps": 0, "xo": 0}

        W = WStream(K, wslots, wsb)
        for t in range(NT):
            for gb in range(4):
                W.add("gm", [(_full, wv(win_d, 0, 16, C_GM + gb * 512, C_GM + (gb + 1) * 512))], (16, 512))
                W.add("bm", [(_full, wv(wbm_d, 0, 8, gb * 512, (gb + 1) * 512))], (8, 512))
                W.add("gd", [(_full, wv(win_d, 0, 16, C_GD + gb * 512, C_GD + (gb + 1) * 512))], (16, 512))
                W.add("bd", [(_full, wv(wbd_d, 0, 8, gb * 512, (gb + 1) * 512))], (8, 512))
            for ob in range(4):
                W.add("wo", [(_full, wv(wo_d, 0, 16, ob * 512, (ob + 1) * 512))], (16, 512))
            ffn_plan(W, w2g_d, w2u_d, w2d_d)

        def nps():
            i = sctr["ps"] % 8
            sctr["ps"] += 1
            return i

        def load_inputs(t):
            t0 = t * TT
            K.op("sp", lambda e: e.dma_start(out=B.xb[:, :, :], in_=H1B[:, :, t0:t0 + TT]), writes=B.xbb, dma=1, key="l_h1b")
            K.op("sp", lambda e: e.dma_start(out=om[:, :, :], in_=OM[:, :, t0:t0 + TT]), writes=[omb], dma=1, key="l_om")
            K.op("sp", lambda e: e.dma_start(out=odt[:, :, :], in_=OD[:, :, t0:t0 + TT]), writes=[odtb], dma=1, key="l_od")

        def tile3(t):
            tok0 = t * TT
            if t == 0:
                load_inputs(0)
            K.op("sp", lambda e: e.dma_start(out=B.xa[:, :, :], in_=H1A[:, :, tok0:tok0 + TT]), writes=B.xab, dma=1, key="l_h1a")
            for gb in range(4):
                for half, (gt, bt, osrc, osrcb) in enumerate((("gm", "bm", om, omb), ("gd", "bd", odt, odtb))):
                    wg_, wgb_ = W.next(gt)
                    wb_, wbb_ = W.next(bt)
                    for c in range(4):
                        yc = gb * 4 + c
                        pg, pb_ = nps(), nps()
                        for kc in range(KC):
                            K.op("pe", lambda e, pg=pg, kc=kc, c=c, wg_=wg_: e.matmul(
                                psum[pg][:, :], lhsT=wg_[:, kc, c * 128:(c + 1) * 128], rhs=B.xb[:, kc, :],
                                start=(kc == 0), stop=(kc == KC - 1)), reads=[wgb_, B.xbb[kc]], writes=[psb[pg]])
                        for kc in range(8):
                            K.op("pe", lambda e, pb_=pb_, kc=kc, c=c, wb_=wb_, osrc=osrc: e.matmul(
                                psum[pb_][:, :], lhsT=wb_[:, kc, c * 128:(c + 1) * 128], rhs=osrc[:, kc, :],
                                start=(kc == 0), stop=(kc == 7)), reads=[wbb_, osrcb], writes=[psb[pb_]])
                        ti = B.ctr["f"] % 4
                        B.ctr["f"] += 1
                        tf, tfb = B.tmpf[ti], B.tmpfb[ti]
                        K.op("act", lambda e, pg=pg, tf=tf: e.activation(out=tf[:, :], in_=psum[pg][:, :], func=AF.Sigmoid),
                             reads=[psb[pg]], writes=[tfb])
                        if half == 0:
                            K.op("dve", lambda e, pb_=pb_, tf=tf, c=c: e.tensor_tensor(
                                out=ytmp[:, c, :], in0=tf[:, :], in1=psum[pb_][:, :], op=ALU.mult),
                                reads=[tfb, psb[pb_]], writes=[ytmpb[c]])
                        else:
                            K.op("dve", lambda e, pb_=pb_, tf=tf: e.tensor_tensor(
                                out=tf[:, :], in0=tf[:, :], in1=psum[pb_][:, :], op=ALU.mult),
                                reads=[tfb, psb[pb_]], writes=[tfb])
                            K.op("dve", lambda e, tf=tf, c=c, yc=yc: e.tensor_tensor(
                                out=yT[:, yc, :], in0=tf[:, :], in1=ytmp[:, c, :], op=ALU.add),
                                reads=[tfb, ytmpb[c]], writes=[yb[yc]])
                    W.release(2)
            pend = []
            for ob in range(4):
                wo_, wob_ = W.next("wo")
                for c in range(4):
                    dc = ob * 4 + c
                    pi = nps() % 4
                    for kc in range(KC):
                        K.op("pe", lambda e, pi=pi, kc=kc, c=c, wo_=wo_: e.matmul(
                            psum[pi][:, :], lhsT=wo_[:, kc, c * 128:(c + 1) * 128], rhs=yT[:, kc, :],
                            start=(kc == 0), stop=(kc == KC - 1)), reads=[wob_, yb[kc]], writes=[psb[pi]])
                    while pend:
                        tb1, tb2, first, last = pend.pop(0)
                        K.op("pe", lambda e, tb1=tb1, first=first, last=last: e.matmul(
                            psum[4][:, :], lhsT=ones_bf[:, :], rhs=B.tmpb[tb1][:, :], start=first, stop=last),
                            reads=[B.tmpbb[tb1], b_const], writes=[psb[4]])
                        K.op("pe", lambda e, tb2=tb2, first=first, last=last: e.matmul(
                            psum[5][:, :], lhsT=ones_bf[:, :], rhs=B.tmpb[tb2][:, :], start=first, stop=last),
                            reads=[B.tmpbb[tb2], b_const], writes=[psb[5]])
                    K.op("dve", lambda e, pi=pi, dc=dc: e.tensor_tensor(out=B.xa[:, dc, :], in0=psum[pi][:, :], in1=B.xa[:, dc, :],
                                                                        op=ALU.add),
                         reads=[psb[pi], B.xab[dc]], writes=[B.xab[dc]])
                    tb1 = B.ctr["b"] % 4
                    tb2 = (B.ctr["b"] + 1) % 4
                    B.ctr["b"] += 2
                    K.op("act", lambda e, tb1=tb1, dc=dc: e.activation(out=B.tmpb[tb1][:, :], in_=B.xa[:, dc, :], func=AF.Copy),
                         reads=[B.xab[dc]], writes=[B.tmpbb[tb1]])
                    K.op("act", lambda e, tb2=tb2, dc=dc: e.activation(out=B.tmpb[tb2][:, :], in_=B.xa[:, dc, :], func=AF.Square),
                         reads=[B.xab[dc]], writes=[B.tmpbb[tb2]])
                    pend.append((tb1, tb2, dc == 0, dc == KC - 1))
                W.release(1)
            while pend:
                tb1, tb2, first, last = pend.pop(0)
                K.op("pe", lambda e, tb1=tb1, first=first, last=last: e.matmul(
                    psum[4][:, :], lhsT=ones_bf[:, :], rhs=B.tmpb[tb1][:, :], start=first, stop=last),
                    reads=[B.tmpbb[tb1], b_const], writes=[psb[4]])
                K.op("pe", lambda e, tb2=tb2, first=first, last=last: e.matmul(
                    psum[5][:, :], lhsT=ones_bf[:, :], rhs=B.tmpb[tb2][:, :], start=first, stop=last),
                    reads=[B.tmpbb[tb2], b_const], writes=[psb[5]])
            ln_tail(B, V_LN2G, V_LN2B, V_LN2G, V_LN2B, True, True)
            ffn(B, W, (lambda: load_inputs(t + 1)) if t + 1 < NT else None)
            ln_tail(B, V_LN3G, V_LN3B, V_LN3G, V_LN3B, False, True, final_plain=True)
            for s in range(4):
                for q4 in range(4):
                    xi = sctr["xo"] % 2
                    sctr["xo"] += 1
                    pi = nps()
                    for j in range(4):
                        dc = q4 * 4 + j
                        K.op("pe", lambda e, pi=pi, j=j, dc=dc, s=s: e.transpose(
                            out=psum[pi][:, j * 128:(j + 1) * 128], in_=B.xa[:, dc, s * 128:(s + 1) * 128],
                            identity=ident[:, :]), reads=[B.xab[dc], b_const], writes=[psb[pi]])
                    if q4 % 2:
                        K.op("act", lambda e, pi=pi, xi=xi: e.activation(out=xout[xi][:, :], in_=psum[pi][:, :], func=AF.Copy),
                             reads=[psb[pi]], writes=[xoutb[xi]])
                    else:
                        K.op("dve", lambda e, pi=pi, xi=xi: e.tensor_copy(out=xout[xi][:, :], in_=psum[pi][:, :]),
                             reads=[psb[pi]], writes=[xoutb[xi]])
                    K.op("sp", lambda e, xi=xi, s=s, q4=q4: e.dma_start(
                        out=out_d[tok0 + s * 128:tok0 + (s + 1) * 128, q4 * 512:(q4 + 1) * 512], in_=xout[xi][:, :]),
                        reads=[xoutb[xi]], dma=1, key=("xout", xi))

        for t in range(NT):
            tile3(t)
        K.end_phase()

    es.close()
    return nc, dbg


def make_core_inputs(inputs):
    x = np.asarray(inputs["x"], dtype=np.float32)
    pos = np.asarray(inputs["positions"]).astype(np.int32)
    g = lambda k: np.asarray(inputs[k], dtype=np.float32)[0]

    vecs = np.zeros((128, NV), np.float32)

    def put(col, v, n):
        vecs[:, col:col + n] = v.reshape(n, 128).T

    put(V_LN1G, g("ln1_g"), 16)
    put(V_LN1B, g("ln1_b"), 16)
    put(V_LN2G, g("ln2_g"), 16)
    put(V_LN2B, g("ln2_b"), 16)
    put(V_LN3G, g("ln3_g"), 16)
    put(V_LN3B, g("ln3_b"), 16)
    put(V_QG, g("mla_q_norm_g"), 4)
    put(V_KVG, g("mla_kv_norm_g"), 4)
    put(V_SUBG, g("diff_subln_g"), 2)
    for i, k in enumerate(("diff_lambda_q1", "diff_lambda_k1", "diff_lambda_q2", "diff_lambda_k2")):
        put(V_LAM + i, g(k), 1)
    half = 32
    inv_freq = (10000.0 ** (-np.arange(half, dtype=np.float32) / half)).astype(np.float32)
    vecs[0:64, V_INVF] = np.concatenate([inv_freq, inv_freq])
    vecs[0:64, V_SGN] = np.concatenate([-np.ones(32, np.float32), np.ones(32, np.float32)])
    ident = np.eye(128, dtype=np.float32)

    wnames = ["ffn1_w_gate", "ffn1_w_up", "ffn1_w_down", "w_in", "mla_w_uq", "mla_w_uk", "mla_w_uv",
              "w_branch_mla", "w_branch_diff", "w_out", "ffn2_w_gate", "ffn2_w_up", "ffn2_w_down"]
    wts = {k: np.ascontiguousarray(g(k)) for k in wnames}
    maps = []
    tokidx = []
    for c in range(8):
        b, r = c // 2, c % 2
        ti = np.concatenate([np.arange(t * TT, (t + 1) * TT) for t in TILES[r]])
        tokidx.append(ti)
    for c in range(8):
        b, r = c // 2, c % 2
        ti = tokidx[c]
        kg = np.concatenate([tokidx[2 * b], tokidx[2 * b + 1]])
        m = dict(wts)
        m["x"] = np.ascontiguousarray(x[b, ti, :])
        m["qposb"] = np.ascontiguousarray(np.broadcast_to(pos[b, ti][None, :], (128, NTOK))).astype(np.int32)
        m["kposc"] = np.ascontiguousarray(pos[b, kg].reshape(32, 128).T).astype(np.int32)
        m["qidxb"] = np.ascontiguousarray(np.broadcast_to(ti[None, :].astype(np.float32), (128, NTOK)))
        m["kidxc"] = np.ascontiguousarray(kg.astype(np.float32).reshape(32, 128).T)
        m["vecs"] = vecs
        m["ident"] = ident
        maps.append(m)
    return maps, tokidx


_CACHE = {}


def kernel(**inputs):
    maps, tokidx = make_core_inputs(inputs)
    if "nc" not in _CACHE:
        _CACHE["nc"] = build_program(False)[0]
    nc = _CACHE["nc"]
    res = run_bass_kernel_spmd(nc, maps, core_ids=list(range(8)))
    out = np.empty((4, SEQ, D), np.float32)
    for c in range(8):
        out[c // 2, tokidx[c], :] = res.results[c]["out"]
    return out
```
